# Optimizing a Trainium2 kernel written in Bass

```python
import jax, jax.numpy as jnp
from jax import lax
import numpy as np

D_MODEL = 2048
BATCH = 2
SEQ = 16384
DEPTH = 2

CONV_DIM = 512
CONV_K = 3
MLA_HEADS = 8
QK_NOPE = 128
QK_ROPE = 64
QK_HEAD = QK_NOPE + QK_ROPE
V_DIM = 128
MLA_DIM = MLA_HEADS * V_DIM
Q_LORA = 512
KV_LORA = 256
ROPE_THETA = 10000.0
Q_BLOCK = 128
RWKV_HEADS = 8
RWKV_N = 64
RWKV_DIM = RWKV_HEADS * RWKV_N
DECAY_LORA = 64
A_LORA = 64
RWKV_SHIFT_SIZES = (RWKV_DIM, DECAY_LORA, RWKV_DIM, RWKV_DIM, A_LORA)
RWKV_SHIFT_DIM = 3 * RWKV_DIM + DECAY_LORA + A_LORA
RWKV_GN_EPS = 64e-5
MIX_DIM = CONV_DIM + MLA_DIM + RWKV_DIM
IN_SIZES = (CONV_DIM, CONV_DIM, CONV_DIM, CONV_DIM,
            Q_LORA, KV_LORA, QK_ROPE, MLA_DIM,
            RWKV_SHIFT_DIM, RWKV_DIM)
IN_TOTAL = sum(IN_SIZES)
LN_EPS = 1e-5
RMS_EPS = 1e-6
DEEPNORM_ALPHA = (2 * DEPTH) ** 0.25
DEEPNORM_BETA = (8 * DEPTH) ** -0.25

kernel_name = "hybrid_conv_mla_rwkv7_deepnorm"


def _split(u, sizes):
    idx, acc = [], 0
    for s in sizes[:-1]:
        acc += s
        idx.append(acc)
    return jnp.split(u, idx, axis=-1)


def _layer_norm(x, g, b):
    xf = x.astype(jnp.float32)
    mu = jnp.mean(xf, -1, keepdims=True)
    var = jnp.mean(jnp.square(xf - mu), -1, keepdims=True)
    return ((xf - mu) * lax.rsqrt(var + LN_EPS)).astype(x.dtype) * g + b


def _rms_norm(x, g):
    xf = x.astype(jnp.float32)
    return (xf * lax.rsqrt(jnp.mean(xf * xf, -1, keepdims=True) + RMS_EPS)).astype(x.dtype) * g


def _causal_shift(u, n):
    return jnp.pad(u, ((0, 0), (n, 0), (0, 0)))[:, :u.shape[1]]


def _rope(x, cos, sin):
    x1, x2 = jnp.split(x, 2, axis=-1)
    return jnp.concatenate([x1 * cos - x2 * sin, x2 * cos + x1 * sin], axis=-1)


def _short_conv_branch(b_gate, c_gate, h, gate, conv_w):
    u = c_gate * h
    S = u.shape[1]
    up = jnp.pad(u, ((0, 0), (CONV_K - 1, 0), (0, 0)))
    y = sum(conv_w[j] * up[:, j:j + S] for j in range(CONV_K))
    return b_gate * y * jax.nn.silu(gate)


def _causal_block_attention(q, k, v):
    B, S, H, D = q.shape
    n_blk = S // Q_BLOCK
    scale = D ** -0.5
    q_blocks = q.reshape(B, n_blk, Q_BLOCK, H, D).transpose(1, 0, 2, 3, 4)
    k_pos = jnp.arange(S)

    def one_block(args):
        qb, blk = args
        s = jnp.einsum('bqhd,bkhd->bhqk', qb, k, preferred_element_type=jnp.float32) * scale
        q_pos = blk * Q_BLOCK + jnp.arange(Q_BLOCK)
        s = jnp.where(k_pos[None, :] <= q_pos[:, None], s, -jnp.inf)
        p = jax.nn.softmax(s, axis=-1).astype(v.dtype)
        return jnp.einsum('bhqk,bkhd->bqhd', p, v)

    o = lax.map(one_block, (q_blocks, jnp.arange(n_blk)))
    return o.transpose(1, 0, 2, 3, 4).reshape(B, S, H, v.shape[-1])


def _mla_branch(c_q, c_kv, k_pe, gate, q_norm_g, w_uq, kv_norm_g, w_ukv, cos, sin):
    B, S, _ = c_q.shape
    q = (_rms_norm(c_q, q_norm_g) @ w_uq).reshape(B, S, MLA_HEADS, QK_HEAD)
    q_nope, q_pe = q[..., :QK_NOPE], q[..., QK_NOPE:]
    kv = (_rms_norm(c_kv, kv_norm_g) @ w_ukv).reshape(B, S, MLA_HEADS, QK_NOPE + V_DIM)
    k_nope, v = kv[..., :QK_NOPE], kv[..., QK_NOPE:]
    q = jnp.concatenate([q_nope, _rope(q_pe, cos, sin)], axis=-1)
    k_pe = _rope(k_pe[:, :, None, :], cos, sin)
    k = jnp.concatenate([k_nope, jnp.broadcast_to(k_pe, (B, S, MLA_HEADS, QK_ROPE))], axis=-1)
    o = _causal_block_attention(q, k, v)
    return o.reshape(B, S, MLA_DIM) * jax.nn.silu(gate)


def _rwkv7_scan(r, decay, k, v, a_vec, b_vec):
    B, S, H, N = r.shape
    xs = tuple(t.astype(jnp.float32).transpose(1, 0, 2, 3) for t in (r, decay, k, v, a_vec, b_vec))

    def step(state, inp):
        r_t, w_t, k_t, v_t, a_t, b_t = inp
        sa = jnp.einsum('bhvk,bhk->bhv', state, a_t)
        state = (state * w_t[:, :, None, :] + sa[..., None] * b_t[:, :, None, :]
                 + v_t[..., None] * k_t[:, :, None, :])
        return state, jnp.einsum('bhvk,bhk->bhv', state, r_t)

    s0 = jnp.zeros((B, H, N, N), jnp.float32)
    _, y = lax.scan(step, s0, xs)
    return y.transpose(1, 0, 2, 3).astype(r.dtype)


def _rwkv7_branch(shift_cols, gate, mu, w0, w2, a0, a2, k_k, k_a, r_k, gn_g, gn_b):
    B, S, _ = shift_cols.shape
    xs = shift_cols + (_causal_shift(shift_cols, 1) - shift_cols) * mu
    r, wd, k, v, ad = _split(xs, RWKV_SHIFT_SIZES)
    w = -jax.nn.softplus(-(w0 + jnp.tanh(wd) @ w2)) - 0.5
    decay = jnp.exp(-jnp.exp(w.astype(jnp.float32)))
    a = jax.nn.sigmoid(a0 + ad @ a2)
    kk = (k * k_k).reshape(B, S, RWKV_HEADS, RWKV_N).astype(jnp.float32)
    kk = (kk / jnp.maximum(jnp.linalg.norm(kk, axis=-1, keepdims=True), 1e-12)).astype(k.dtype)
    k = k * (1 + (a - 1) * k_a)
    hs = lambda t: t.reshape(B, S, RWKV_HEADS, RWKV_N)
    r, k, v, a, decay = hs(r), hs(k), hs(v), hs(a), hs(decay)
    y = _rwkv7_scan(r, decay, k, v, -kk, kk * a)
    yf = y.astype(jnp.float32)
    ym = jnp.mean(yf, -1, keepdims=True)
    yv = jnp.mean(jnp.square(yf - ym), -1, keepdims=True)
    y = ((yf - ym) * lax.rsqrt(yv + RWKV_GN_EPS)).astype(v.dtype)
    y = y * gn_g.reshape(RWKV_HEADS, RWKV_N) + gn_b.reshape(RWKV_HEADS, RWKV_N)
    y = y + jnp.sum(r * k * r_k, axis=-1, keepdims=True) * v
    return y.reshape(B, S, RWKV_DIM) * jax.nn.silu(gate)


def setup_inputs(seed: int = 0) -> dict:
    key = jax.random.key(seed)
    ks = jax.random.split(key, 24)
    n = lambda i, shape: jax.random.normal(ks[i], shape, jnp.float32)
    L = DEPTH
    x = n(0, (BATCH, SEQ, D_MODEL))
    positions = (jnp.arange(SEQ, dtype=jnp.int32)[None, :]
                 + jax.random.randint(ks[1], (BATCH, 1), 0, 4096, dtype=jnp.int32))
    return {
        "x": x,
        "positions": positions,
        "w_in": n(2, (L, D_MODEL, IN_TOTAL)) * D_MODEL ** -0.5,
        "conv_w": n(3, (L, CONV_K, CONV_DIM)) * CONV_K ** -0.5,
        "q_norm_g": 1.0 + 0.02 * n(4, (L, Q_LORA)),
        "w_uq": n(5, (L, Q_LORA, MLA_HEADS * QK_HEAD)) * Q_LORA ** -0.5,
        "kv_norm_g": 1.0 + 0.02 * n(6, (L, KV_LORA)),
        "w_ukv": n(7, (L, KV_LORA, MLA_HEADS * (QK_NOPE + V_DIM))) * KV_LORA ** -0.5,
        "rwkv_mu": jax.random.uniform(ks[8], (L, RWKV_SHIFT_DIM), jnp.float32),
        "rwkv_w0": 0.3 * n(9, (L, RWKV_DIM)),
        "rwkv_w2": n(10, (L, DECAY_LORA, RWKV_DIM)) * 0.5 * DECAY_LORA ** -0.5,
        "rwkv_a0": 0.3 * n(11, (L, RWKV_DIM)),
        "rwkv_a2": n(12, (L, A_LORA, RWKV_DIM)) * 0.5 * A_LORA ** -0.5,
        "rwkv_k_k": 0.85 + 0.05 * n(13, (L, RWKV_DIM)),
        "rwkv_k_a": 1.0 + 0.05 * n(14, (L, RWKV_DIM)),
        "rwkv_r_k": 0.1 * n(15, (L, RWKV_HEADS, RWKV_N)),
        "rwkv_gn_g": 1.0 + 0.02 * n(16, (L, RWKV_DIM)),
        "rwkv_gn_b": 0.02 * n(17, (L, RWKV_DIM)),
        "w_out": n(18, (L, MIX_DIM, D_MODEL)) * MIX_DIM ** -0.5 * DEEPNORM_BETA,
        "ln_g": 1.0 + 0.02 * n(19, (L, D_MODEL)),
        "ln_b": 0.02 * n(20, (L, D_MODEL)),
    }


def reference(x, positions, w_in, conv_w, q_norm_g, w_uq, kv_norm_g, w_ukv,
              rwkv_mu, rwkv_w0, rwkv_w2, rwkv_a0, rwkv_a2, rwkv_k_k, rwkv_k_a,
              rwkv_r_k, rwkv_gn_g, rwkv_gn_b, w_out, ln_g, ln_b):
    inv_freq = ROPE_THETA ** (-jnp.arange(0, QK_ROPE, 2, dtype=jnp.float32) / QK_ROPE)
    ang = positions.astype(jnp.float32)[..., None] * inv_freq
    cos = jnp.cos(ang)[:, :, None, :].astype(x.dtype)
    sin = jnp.sin(ang)[:, :, None, :].astype(x.dtype)

    for l in range(DEPTH):
        u = x @ w_in[l]
        (cb, cc, ch, cg, cq, ckv, kpe, mg, rcols, rg) = _split(u, IN_SIZES)
        y_conv = _short_conv_branch(cb, cc, ch, cg, conv_w[l])
        y_mla = _mla_branch(cq, ckv, kpe, mg, q_norm_g[l], w_uq[l],
                            kv_norm_g[l], w_ukv[l], cos, sin)
        y_rwkv = _rwkv7_branch(rcols, rg, rwkv_mu[l], rwkv_w0[l], rwkv_w2[l],
                               rwkv_a0[l], rwkv_a2[l], rwkv_k_k[l], rwkv_k_a[l],
                               rwkv_r_k[l], rwkv_gn_g[l], rwkv_gn_b[l])
        mix = jnp.concatenate([y_conv, y_mla, y_rwkv], axis=-1)
        x = _layer_norm(DEEPNORM_ALPHA * x + mix @ w_out[l], ln_g[l], ln_b[l])
    return x
```

```python
import contextlib
import math
import types
import numpy as np
import ml_dtypes
import concourse.bass as bass
import concourse.mybir as mybir
from concourse.bass_utils import run_bass_kernel_spmd

F32 = mybir.dt.float32
BF16 = mybir.dt.bfloat16
I32 = mybir.dt.int32
AF = mybir.ActivationFunctionType
ALU = mybir.AluOpType

D_MODEL = 2048
NCORES = 8
import os
_DBG_LEVEL = int(os.environ.get('DBG_LEVEL', '3'))
ENGS = ("pe", "act", "dve", "pool", "sp")


def _snap(fn):
    if fn.__closure__ is None:
        return fn
    cells = []
    for c in fn.__closure__:
        try:
            cells.append(types.CellType(c.cell_contents))
        except ValueError:
            cells.append(c)
    return types.FunctionType(fn.__code__, fn.__globals__, fn.__name__, fn.__defaults__, tuple(cells))


class Buf:
    __slots__ = ("name", "w", "r", "excl")

    def __init__(self, name, excl=False):
        self.name = name
        self.w = None
        self.r = {}
        self.excl = excl


class KB:
    def __init__(self, nc, es):
        self.nc = nc
        self.es = es
        self.rec = {e: [] for e in ENGS}
        self.cnt = {e: 0 for e in ENGS}
        self.seen = {e: {} for e in ENGS}
        self.sems = {}
        self.dcnt = {}
        self.nsem = 0
        for e in ENGS:
            if e != "sp":
                self.sems[e] = es.enter_context(nc.semaphore("s_" + e))
        self._uid = 0
        self.out_events = []

    def uid(self, p):
        self._uid += 1
        return "%s%d" % (p, self._uid)

    def buf(self, name="b"):
        return Buf(name)

    def pbuf(self, name="p"):
        return Buf(name, excl=True)

    def dsem(self, name):
        key = ("d", self.uid(name))
        self.sems[key] = self.es.enter_context(self.nc.semaphore(key[1]))
        self.dcnt[key] = 0
        return key

    def sb(self, shape, dtype, name="t"):
        return self.es.enter_context(self.nc.sbuf_tensor(self.uid(name), list(shape), dtype))

    def ps(self, shape, dtype=F32, name="p"):
        return self.es.enter_context(self.nc.psum_tensor(self.uid(name), list(shape), dtype))

    def _deps(self, eng, reads, writes):
        evs = {}

        def add(ev, kind):
            if ev is None:
                return
            key, val = ev
            if key == eng and eng == "pe":
                return
            if evs.get(key, 0) < val:
                evs[key] = val

        for b in reads:
            add(b.w, "raw")
            if b.excl:
                for key, val in b.r.items():
                    if key != eng:
                        add((key, val), "rar")
        for b in writes:
            add(b.w, "waw")
            for key, val in b.r.items():
                add((key, val), "war")
        waits = []
        seen = self.seen[eng]
        for key, val in evs.items():
            if seen.get(key, 0) >= val:
                continue
            seen[key] = val
            waits.append((self.sems[key], val))
        return waits

    def _post(self, ev, reads, writes):
        key, val = ev
        for b in reads:
            if b.r.get(key, 0) < val:
                b.r[key] = val
        for b in writes:
            b.w = ev
            b.r = {}

    def op(self, eng, fn, reads=(), writes=()):
        fn = _snap(fn)
        waits = self._deps(eng, reads, writes)
        self.cnt[eng] += 1
        idx = self.cnt[eng]
        sem = self.sems[eng]

        def run(e, waits=waits, fn=fn, sem=sem):
            for s, v in waits:
                e.wait_ge(s, v)
            fn(e).then_inc(sem, 1)

        self.rec[eng].append(run)
        ev = (eng, idx)
        self._post(ev, reads, writes)
        return ev

    def dma(self, q, dkey, fn, reads=(), writes=(), is_output=False):
        fn = _snap(fn)
        waits = self._deps(q, reads, writes)
        self.dcnt[dkey] += 16
        val = self.dcnt[dkey]
        sem = self.sems[dkey]

        def run(e, waits=waits, fn=fn, sem=sem):
            for s, v in waits:
                e.wait_ge(s, v)
            fn(e).then_inc(sem, 16)

        self.rec[q].append(run)
        ev = (dkey, val)
        self._post(ev, reads, writes)
        if is_output:
            self.out_events.append(ev)
        return ev

    def finish(self):
        last = {}
        for key, val in self.out_events:
            if last.get(key, 0) < val:
                last[key] = val
        fin = [(self.sems[k], v) for k, v in last.items()]

        def run(e, fin=fin):
            for s, v in fin:
                e.wait_ge(s, v)

        self.rec["sp"].append(run)
        rec = self.rec
        with self.nc.Block() as block:
            @block.sync
            def _(e):
                for r in rec["sp"]:
                    r(e)

            @block.tensor
            def _(e):
                for r in rec["pe"]:
                    r(e)

            @block.scalar
            def _(e):
                for r in rec["act"]:
                    r(e)

            @block.vector
            def _(e):
                for r in rec["dve"]:
                    r(e)

            @block.gpsimd
            def _(e):
                for r in rec["pool"]:
                    r(e)


class Rot:
    def __init__(self, items):
        self.items = items
        self.i = 0

    def next(self):
        it = self.items[self.i % len(self.items)]
        self.i += 1
        return it


def sb_rot(k, n, shape, dtype, name):
    return Rot([(k.sb(shape, dtype, name), k.buf(name)) for _ in range(n)])


TWO_PI = 2.0 * math.pi


def _split_2pi():
    def trunc_bits(v, bits):
        m, e = math.frexp(v)
        m = math.floor(m * (1 << bits)) / (1 << bits)
        return math.ldexp(m, e)
    c1 = trunc_bits(TWO_PI, 8)
    c2 = trunc_bits(TWO_PI - c1, 11)
    c3 = float(np.float32(TWO_PI - c1 - c2))
    return float(c1), float(c2), c3


C1, C2, C3 = _split_2pi()

P1_BLOCKS = [
    ("cB", 128), ("cC", 128), ("ch", 128), ("cg", 128),
    ("cq0", 128), ("cq1", 128), ("cq2", 128), ("cq3", 128),
    ("ckv0", 128), ("ckv1", 128), ("kpe", 64), ("kpr", 64),
    ("mg0", 128), ("mg1", 128),
    ("rr", 128), ("rk", 128), ("rv", 128), ("rwa", 128), ("rg", 128),
]
P1_OFF = {}
_o = 0
for _n, _w in P1_BLOCKS:
    P1_OFF[_n] = (_o, _w)
    _o += _w
P1_NC = _o

PP = {n: i for i, n in enumerate([
    "cw0", "cw1", "cw2", "qg0", "qg1", "qg2", "qg3", "kvg0", "kvg1",
    "mu_r", "mu_k", "mu_v", "mu_wa", "w0", "a0", "k_k", "k_a", "r_k", "gn_g", "gn_b", "invf",
    "om_r", "om_k", "om_v", "om_wa", "omk_a", "sgn",
])}
NPP = len(PP)
NPP_IN = PP["om_r"]


def build_p1(k, S, d, TT=512, stage=None):
    nc = k.nc
    NT = S // TT
    KC = D_MODEL // 128
    xT, w1, pp_d, wuq_d, wukv_d, w2a2_d, pos_d = d["xT"], d["w1"], d["pp"], d["wuq"], d["wukv"], d["w2a2"], d["pos"]

    pp = k.sb([128, NPP], F32, "pp")
    b_pp = k.buf("pp")
    ds_c = k.dsem("c")
    k.dma("sp", ds_c, lambda e: e.dma_start(out=pp[:, 0:NPP_IN], in_=pp_d[:, :]), writes=[b_pp])

    def ppc(name, lo=0, hi=128):
        return pp[lo:hi, PP[name]:PP[name] + 1]

    for src, dst in (("mu_r", "om_r"), ("mu_k", "om_k"), ("mu_v", "om_v"), ("mu_wa", "om_wa"), ("k_a", "omk_a")):
        k.op("dve", lambda e, s=src, t=dst: e.tensor_scalar(out=ppc(t), in0=ppc(s), scalar1=-1.0, scalar2=1.0,
                                                             op0=ALU.mult, op1=ALU.add), reads=[b_pp], writes=[b_pp])
    k.op("dve", lambda e: e.memset(pp[0:32, PP["sgn"]:PP["sgn"] + 1], -1.0), writes=[b_pp])
    k.op("dve", lambda e: e.memset(pp[32:64, PP["sgn"]:PP["sgn"] + 1], 1.0), writes=[b_pp])

    ones_bf = k.sb([128, 128], BF16, "ones")
    b_ones = k.buf("ones")
    k.op("dve", lambda e: e.memset(ones_bf[:], 1.0), writes=[b_ones])
    eps_t = k.sb([128, 1], F32, "eps")
    b_eps = k.buf("eps")
    k.op("dve", lambda e: e.memset(eps_t[:], 1e-6), writes=[b_eps])
    bones_bf = k.sb([128, 128], BF16, "bones")
    b_bones = k.buf("bones")
    k.op("dve", lambda e: e.memset(bones_bf[:], 0.0), writes=[b_bones])
    k.op("dve", lambda e: e.memset(bones_bf[0:64, 0:64], 1.0), writes=[b_bones])
    k.op("dve", lambda e: e.memset(bones_bf[64:128, 64:128], 1.0), writes=[b_bones])

    wbf = k.sb([128, KC, P1_NC], BF16, "wbf")
    b_wbf = k.buf("wbf")
    HW = P1_NC // 2
    stg = [(k.sb([128, HW], F32, "wstg"), k.buf("wstg"), k.dsem("wst")) for _ in range(2)]
    for c in range(KC):
        for hh in range(2):
            st, bst, dss = stg[hh]
            k.dma("sp", dss, lambda e, c=c, st=st, hh=hh: e.dma_start(out=st[:], in_=w1[c * 128:(c + 1) * 128, hh * HW:(hh + 1) * HW]), writes=[bst])
            eng = "dve" if hh == 0 else "pool"
            k.op(eng, lambda e, c=c, st=st, hh=hh: e.tensor_copy(out=wbf[:, c, hh * HW:(hh + 1) * HW], in_=st[:]), reads=[bst], writes=[b_wbf])

    wuq = k.sb([128, 4, 512], BF16, "wuq")
    b_wuq = k.buf("wuq")
    qscale = 192.0 ** -0.5
    for c in range(4):
        st, bst, dss = stg[c % 2]
        k.dma("sp", dss, lambda e, c=c, st=st: e.dma_start(out=st[:, 0:512], in_=wuq_d[c * 128:(c + 1) * 128, :]), writes=[bst])
        k.op("dve", lambda e, c=c, st=st: e.tensor_scalar(out=wuq[:, c, :], in0=st[:, 0:512], scalar1=ppc("qg%d" % c),
                                                          scalar2=qscale, op0=ALU.mult, op1=ALU.mult),
             reads=[bst, b_pp], writes=[b_wuq])
    wukv = k.sb([128, 2, 512], BF16, "wukv")
    b_wukv = k.buf("wukv")
    for c in range(2):
        st, bst, dss = stg[c % 2]
        k.dma("sp", dss, lambda e, c=c, st=st: e.dma_start(out=st[:, 0:512], in_=wukv_d[c * 128:(c + 1) * 128, :]), writes=[bst])
        k.op("dve", lambda e, c=c, st=st: e.tensor_scalar(out=wukv[:, c, :], in0=st[:, 0:512], scalar1=ppc("kvg%d" % c),
                                                          scalar2=None, op0=ALU.mult), reads=[bst, b_pp], writes=[b_wukv])
    w2a2 = k.sb([128, 128], BF16, "w2a2")
    b_w2a2 = k.buf("w2a2")
    st, bst, dss = stg[0]
    k.dma("sp", dss, lambda e, st=st: e.dma_start(out=st[:, 0:128], in_=w2a2_d[:, :]), writes=[bst])
    k.op("dve", lambda e, st=st: e.tensor_copy(out=w2a2[:], in_=st[:, 0:128]), reads=[bst], writes=[b_w2a2])

    xt_rot = [(k.sb([128, KC, TT], BF16, "xt"), k.buf("xt"), k.dsem("xt")) for _ in range(2)]
    psum = Rot([(k.ps([128, TT], F32, "ps"), k.pbuf("ps")) for _ in range(8)])
    f32t = sb_rot(k, 12, [128, TT], F32, "f")
    cs_rot = sb_rot(k, 4, [64, TT], F32, "cs")
    bf16t = sb_rot(k, 8, [128, TT], BF16, "h")
    obf = [(k.sb([128, TT], BF16, "ob"), k.buf("ob"), k.dsem("ob")) for _ in range(6)]
    of32 = [(k.sb([128, TT], F32, "of"), k.buf("of"), k.dsem("of")) for _ in range(8)]
    obf_i = [0]
    of32_i = [0]

    def next_obf():
        it = obf[obf_i[0] % len(obf)]
        obf_i[0] += 1
        return it

    def next_of32():
        it = of32[of32_i[0] % len(of32)]
        of32_i[0] += 1
        return it

    U1 = (k.sb([128, TT + 2], F32, "U"), k.buf("U"))
    U = [U1, U1]
    RAW = {}
    for n in ("rr", "rk", "rv", "rwa"):
        r1 = (k.sb([128, TT + 1], F32, "raw"), k.buf("raw"))
        RAW[n] = [r1, r1]
    k.op("pool", lambda e: e.memset(U[1][0][:, TT:TT + 2], 0.0), writes=[U[1][1]])
    for n in RAW:
        k.op("pool", lambda e, n=n: e.memset(RAW[n][1][0][:, TT:TT + 1], 0.0), writes=[RAW[n][1][1]])

    posi = k.sb([64, TT], I32, "posi")
    b_posi = k.buf("posi")
    ds_pos = k.dsem("pos")

    def load_x(i):
        xt, bxt, dsx = xt_rot[i % 2]
        for half in range(2):
            k.dma("sp", dsx, lambda e, xt=xt, i=i, half=half: e.dma_start(
                out=xt[:, half * 8:(half + 1) * 8, :],
                in_=xT[half * 1024:(half + 1) * 1024, i * TT:(i + 1) * TT].rearrange("(c p) t -> p c t", p=128)),
                writes=[bxt])

    def mm_block(i, name, xt, bxt):
        off, w = P1_OFF[name]
        pt, bp = psum.next()
        for c in range(KC):
            k.op("pe", lambda e, c=c, pt=pt, off=off, w=w, xt=xt: e.matmul(
                pt[0:w, :], lhsT=wbf[:, c, off:off + w], rhs=xt[:, c, :], start=(c == 0), stop=(c == KC - 1)),
                reads=[b_wbf, bxt], writes=[bp])
        return pt, bp

    def store(q, dkey, dst_ap, t, bt, is_output=True):
        k.dma(q, dkey, lambda e: e.dma_start(out=dst_ap, in_=t), reads=[bt], is_output=is_output)

    if stage == 'weights':
        return
    load_x(0)
    for i in range(NT):
        if i + 1 < NT:
            load_x(i + 1)
        xt, bxt, dsx = xt_rot[i % 2]
        tsl = slice(i * TT, (i + 1) * TT)

        k.dma("sp", ds_pos, lambda e, tsl=tsl: e.dma_start(out=posi[:], in_=pos_d[0:1, tsl].broadcast_to([64, TT])), writes=[b_posi])
        ang, b_ang = f32t.next()
        k.op("dve", lambda e, ang=ang: e.tensor_copy(out=ang[0:64, :], in_=posi[:]), reads=[b_posi], writes=[b_ang])
        k.op("dve", lambda e, ang=ang: e.tensor_scalar(out=ang[0:64, :], in0=ang[0:64, :], scalar1=ppc("invf", 0, 64), scalar2=None,
                                                       op0=ALU.mult), reads=[b_ang, b_pp], writes=[b_ang])
        kq, b_kq = f32t.next()
        ki, b_ki = f32t.next()
        k.op("dve", lambda e, ang=ang, kq=kq: e.tensor_scalar(out=kq[0:64, :], in0=ang[0:64, :], scalar1=1.0 / TWO_PI, scalar2=None,
                                                              op0=ALU.mult), reads=[b_ang], writes=[b_kq])
        k.op("dve", lambda e, kq=kq, ki=ki: e.tensor_copy(out=ki[0:64, :].bitcast(I32), in_=kq[0:64, :]), reads=[b_kq], writes=[b_ki])
        k.op("dve", lambda e, kq=kq, ki=ki: e.tensor_copy(out=kq[0:64, :], in_=ki[0:64, :].bitcast(I32)), reads=[b_ki], writes=[b_kq])
        red, b_red = f32t.next()
        k.op("dve", lambda e, red=red, ang=ang, kq=kq: e.scalar_tensor_tensor(out=red[0:64, :], in0=kq[0:64, :], scalar=-C1, in1=ang[0:64, :],
                                                                            op0=ALU.mult, op1=ALU.add), reads=[b_ang, b_kq], writes=[b_red])
        k.op("dve", lambda e, red=red, kq=kq: e.scalar_tensor_tensor(out=red[0:64, :], in0=kq[0:64, :], scalar=-C2, in1=red[0:64, :],
                                                                   op0=ALU.mult, op1=ALU.add), reads=[b_red, b_kq], writes=[b_red])
        k.op("dve", lambda e, red=red, kq=kq: e.scalar_tensor_tensor(out=red[0:64, :], in0=kq[0:64, :], scalar=-C3, in1=red[0:64, :],
                                                                   op0=ALU.mult, op1=ALU.add), reads=[b_red, b_kq], writes=[b_red])
        sarg, b_sarg = f32t.next()
        carg, b_carg = f32t.next()

        def wrap(dst, bdst, src, bsrc, shift):
            PI_LO = 3.1415925
            k.op("dve", lambda e: e.tensor_scalar(out=dst[0:64, :], in0=src[0:64, :], scalar1=shift, scalar2=None, op0=ALU.add),
                 reads=[bsrc], writes=[bdst])
            for sgn_, cmp_ in ((-1.0, ALU.is_gt), (1.0, ALU.is_lt)):
                m, bm = f32t.next()
                k.op("dve", lambda e, m=m, sgn_=sgn_, cmp_=cmp_: e.tensor_scalar(out=m[0:64, :], in0=dst[0:64, :], scalar1=-sgn_ * math.pi,
                                                                               scalar2=sgn_ * TWO_PI, op0=cmp_, op1=ALU.mult),
                     reads=[bdst], writes=[bm])
                k.op("dve", lambda e, m=m: e.tensor_tensor(out=dst[0:64, :], in0=dst[0:64, :], in1=m[0:64, :], op=ALU.add),
                     reads=[bdst, bm], writes=[bdst])
            k.op("dve", lambda e: e.tensor_scalar(out=dst[0:64, :], in0=dst[0:64, :], scalar1=PI_LO, scalar2=-PI_LO, op0=ALU.min, op1=ALU.max),
                 reads=[bdst], writes=[bdst])

        wrap(sarg, b_sarg, red, b_red, 0.0)
        wrap(carg, b_carg, red, b_red, math.pi / 2)
        cos2, b_cos2 = cs_rot.next()
        sin2, b_sin2 = cs_rot.next()
        k.op("act", lambda e, sarg=sarg, sin2=sin2: e.activation(out=sin2[0:64, :], in_=sarg[0:64, :], func=AF.Sin,
                                                                  scale=ppc("sgn", 0, 64)), reads=[b_sarg, b_pp], writes=[b_sin2])
        k.op("act", lambda e, carg=carg, cos2=cos2: e.activation(out=cos2[0:64, :], in_=carg[0:64, :], func=AF.Sin),
             reads=[b_carg], writes=[b_cos2])

        if stage == 'rope':
            continue
        pC, bpC = mm_block(i, "cC", xt, bxt)
        Csb, bCsb = f32t.next()
        k.op("act", lambda e, Csb=Csb, pC=pC: e.activation(out=Csb[:], in_=pC[:], func=AF.Copy), reads=[bpC], writes=[bCsb])
        ph, bph = mm_block(i, "ch", xt, bxt)
        Ut, bU = U[i % 2]
        Up, bUp = U[(i + 1) % 2]
        k.op("pool", lambda e, Ut=Ut, Up=Up: e.tensor_copy(out=Ut[:, 0:2], in_=Up[:, TT:TT + 2]), reads=[bUp], writes=[bU])
        k.op("dve", lambda e, Ut=Ut, Csb=Csb, ph=ph: e.tensor_tensor(out=Ut[:, 2:TT + 2], in0=Csb[:], in1=ph[:], op=ALU.mult),
             reads=[bCsb, bph], writes=[bU])
        pB, bpB = mm_block(i, "cB", xt, bxt)
        Bsb, bBsb = f32t.next()
        k.op("act", lambda e, Bsb=Bsb, pB=pB: e.activation(out=Bsb[:], in_=pB[:], func=AF.Copy), reads=[bpB], writes=[bBsb])
        pg, bpg = mm_block(i, "cg", xt, bxt)
        sg, bsg = f32t.next()
        k.op("act", lambda e, sg=sg, pg=pg: e.activation(out=sg[:], in_=pg[:], func=AF.Silu), reads=[bpg], writes=[bsg])
        y, by = f32t.next()
        k.op("dve", lambda e, y=y, Ut=Ut: e.tensor_scalar(out=y[:], in0=Ut[:, 0:TT], scalar1=ppc("cw0"), scalar2=None, op0=ALU.mult),
             reads=[bU, b_pp], writes=[by])
        k.op("dve", lambda e, y=y, Ut=Ut: e.scalar_tensor_tensor(out=y[:], in0=Ut[:, 1:TT + 1], scalar=ppc("cw1"), in1=y[:],
                                                                 op0=ALU.mult, op1=ALU.add), reads=[bU, b_pp, by], writes=[by])
        k.op("dve", lambda e, y=y, Ut=Ut: e.scalar_tensor_tensor(out=y[:], in0=Ut[:, 2:TT + 2], scalar=ppc("cw2"), in1=y[:],
                                                                 op0=ALU.mult, op1=ALU.add), reads=[bU, b_pp, by], writes=[by])
        k.op("pool", lambda e, y=y, Bsb=Bsb: e.tensor_tensor(out=y[:], in0=y[:], in1=Bsb[:], op=ALU.mult), reads=[by, bBsb], writes=[by])
        ob, bob, dso = next_obf()
        k.op("pool", lambda e, y=y, sg=sg, ob=ob: e.tensor_tensor(out=ob[:], in0=y[:], in1=sg[:], op=ALU.mult), reads=[by, bsg], writes=[bob])
        store("sp", dso, d["ycT"][:, tsl], ob[:], bob)

        if stage == 'conv':
            continue
        def rms_normed(names, dim, tagn):
            raws = []
            pss, bpss = psum.next()
            n = len(names)
            for j, nm in enumerate(names):
                pt, bp = mm_block(i, nm, xt, bxt)
                sq, bsq = bf16t.next()
                k.op("act", lambda e, sq=sq, pt=pt: e.activation(out=sq[:], in_=pt[:], func=AF.Square), reads=[bp], writes=[bsq])
                raw, braw = f32t.next()
                k.op("dve", lambda e, raw=raw, pt=pt: e.tensor_copy(out=raw[:], in_=pt[:]), reads=[bp], writes=[braw])
                raws.append((raw, braw))
                if _DBG_LEVEL >= 1:
                    k.op("pe", lambda e, sq=sq, j=j, pss=pss: e.matmul(pss[:], lhsT=ones_bf[:], rhs=sq[:], start=(j == 0), stop=(j == n - 1)),
                         reads=[b_ones, bsq], writes=[bpss])
            if _DBG_LEVEL < 2:
                return []
            rstd, brstd = f32t.next()
            k.op("act", lambda e, rstd=rstd, pss=pss: e.activation(out=rstd[:], in_=pss[:], func=AF.Ln, scale=1.0 / dim, bias=eps_t[:, 0:1]),
                 reads=[bpss, b_eps], writes=[brstd])
            k.op("act", lambda e, rstd=rstd: e.activation(out=rstd[:], in_=rstd[:], func=AF.Exp, scale=-0.5), reads=[brstd], writes=[brstd])
            outs = []
            if _DBG_LEVEL < 3:
                return []
            for raw, braw in raws:
                nb, bnb = bf16t.next()
                k.op("pool", lambda e, nb=nb, raw=raw, rstd=rstd: e.tensor_tensor(out=nb[:], in0=raw[:], in1=rstd[:], op=ALU.mult),
                     reads=[braw, brstd], writes=[bnb])
                outs.append((nb, bnb))
            return outs

        cqn = rms_normed(["cq0", "cq1", "cq2", "cq3"], 512.0, "q")
        if stage == "m1":
            continue
        for h in range(2):
            base = h * 256
            pt, bp = psum.next()
            for c in range(4):
                k.op("pe", lambda e, c=c, pt=pt, base=base: e.matmul(pt[:], lhsT=wuq[:, c, base:base + 128], rhs=cqn[c][0][:],
                                                                    start=(c == 0), stop=(c == 3)), reads=[b_wuq, cqn[c][1]], writes=[bp])
            ob, bob, dso = next_obf()
            k.op("act", lambda e, ob=ob, pt=pt: e.activation(out=ob[:], in_=pt[:], func=AF.Copy), reads=[bp], writes=[bob])
            store("sp", dso, d["QnT"][h, :, tsl], ob[:], bob)
            pa, bpa = psum.next()
            for c in range(4):
                k.op("pe", lambda e, c=c, pa=pa, base=base: e.matmul(pa[0:64, :], lhsT=wuq[:, c, base + 128:base + 192], rhs=cqn[c][0][:],
                                                                    start=(c == 0), stop=(c == 3)), reads=[b_wuq, cqn[c][1]], writes=[bpa])
            pb, bpb = psum.next()
            for c in range(4):
                k.op("pe", lambda e, c=c, pb=pb, base=base: e.matmul(pb[0:64, :], lhsT=wuq[:, c, base + 192:base + 256], rhs=cqn[c][0][:],
                                                                    start=(c == 0), stop=(c == 3)), reads=[b_wuq, cqn[c][1]], writes=[bpb])
            t1, bt1 = f32t.next()
            k.op("dve", lambda e, t1=t1, pa=pa: e.tensor_tensor(out=t1[0:64, :], in0=pa[0:64, :], in1=cos2[0:64, :], op=ALU.mult),
                 reads=[bpa, b_cos2], writes=[bt1])
            t2, bt2 = f32t.next()
            k.op("dve", lambda e, t2=t2, pb=pb: e.tensor_tensor(out=t2[0:64, :], in0=pb[0:64, :], in1=sin2[0:64, :], op=ALU.mult),
                 reads=[bpb, b_sin2], writes=[bt2])
            ob, bob, dso = next_obf()
            k.op("pool", lambda e, ob=ob, t1=t1, t2=t2: e.tensor_tensor(out=ob[0:64, :], in0=t1[0:64, :], in1=t2[0:64, :], op=ALU.add),
                 reads=[bt1, bt2], writes=[bob])
            store("sp", dso, d["QpT"][h, :, tsl], ob[0:64, :], bob)

        if stage == "m2":
            continue
        ckvn = rms_normed(["ckv0", "ckv1"], 256.0, "kv")
        for h in range(2):
            pt, bp = psum.next()
            for c in range(2):
                k.op("pe", lambda e, c=c, pt=pt, h=h: e.matmul(pt[:], lhsT=wukv[:, c, h * 128:(h + 1) * 128], rhs=ckvn[c][0][:],
                                                              start=(c == 0), stop=(c == 1)), reads=[b_wukv, ckvn[c][1]], writes=[bp])
            ob, bob, dso = next_obf()
            k.op("act", lambda e, ob=ob, pt=pt: e.activation(out=ob[:], in_=pt[:], func=AF.Copy), reads=[bp], writes=[bob])
            store("sp", dso, d["KnT"][h, :, tsl], ob[:], bob)
        if stage == "m3":
            continue
        for half in range(2):
            pt, bp = psum.next()
            for sbk in range(2):
                s4 = half * 2 + sbk
                for c in range(2):
                    k.op("pe", lambda e, c=c, pt=pt, s4=s4, sbk=sbk: e.matmul(
                        pt[:, sbk * 256:(sbk + 1) * 256], lhsT=ckvn[c][0][:, s4 * 128:(s4 + 1) * 128], rhs=wukv[:, c, 256:512],
                        start=(c == 0), stop=(c == 1)), reads=[b_wukv, ckvn[c][1]], writes=[bp])
            ob, bob, dso = next_obf()
            k.op("act", lambda e, ob=ob, pt=pt: e.activation(out=ob[:], in_=pt[:], func=AF.Copy), reads=[bp], writes=[bob])
            for h in range(2):
                blk0 = i * (TT // 128) + half * 2
                k.dma("sp", dso, lambda e, ob=ob, h=h, blk0=blk0: e.dma_start(
                    out=d["V"][h, :, blk0:blk0 + 2, :],
                    in_=ob[:].rearrange("t (s h v) -> t s h v", s=2, h=2)[:, :, h, :]), reads=[bob], is_output=True)
        if stage == "m4":
            continue
        pa, bpa = mm_block(i, "kpe", xt, bxt)
        pb, bpb = mm_block(i, "kpr", xt, bxt)
        t1, bt1 = f32t.next()
        k.op("dve", lambda e, t1=t1, pa=pa: e.tensor_tensor(out=t1[0:64, :], in0=pa[0:64, :], in1=cos2[0:64, :], op=ALU.mult),
             reads=[bpa, b_cos2], writes=[bt1])
        t2, bt2 = f32t.next()
        k.op("dve", lambda e, t2=t2, pb=pb: e.tensor_tensor(out=t2[0:64, :], in0=pb[0:64, :], in1=sin2[0:64, :], op=ALU.mult),
             reads=[bpb, b_sin2], writes=[bt2])
        ob, bob, dso = next_obf()
        k.op("pool", lambda e, ob=ob, t1=t1, t2=t2: e.tensor_tensor(out=ob[0:64, :], in0=t1[0:64, :], in1=t2[0:64, :], op=ALU.add),
             reads=[bt1, bt2], writes=[bob])
        store("sp", dso, d["KpT"][:, tsl], ob[0:64, :], bob)
        for h in range(2):
            pt, bp = mm_block(i, "mg%d" % h, xt, bxt)
            of, bof, dsf = next_of32()
            k.op("act", lambda e, of=of, pt=pt: e.activation(out=of[:], in_=pt[:], func=AF.Silu), reads=[bp], writes=[bof])
            store("sp", dsf, d["SgM"][h, :, tsl], of[:], bof)

        if stage == 'mla':
            continue
        xs = {}
        for nm, mu in (("rr", "r"), ("rk", "k"), ("rv", "v"), ("rwa", "wa")):
            pt, bp = mm_block(i, nm, xt, bxt)
            raw, braw = RAW[nm][i % 2]
            rawp, brawp = RAW[nm][(i + 1) % 2]
            k.op("pool", lambda e, raw=raw, rawp=rawp: e.tensor_copy(out=raw[:, 0:1], in_=rawp[:, TT:TT + 1]), reads=[brawp], writes=[braw])
            k.op("act", lambda e, raw=raw, pt=pt: e.activation(out=raw[:, 1:TT + 1], in_=pt[:], func=AF.Copy), reads=[bp], writes=[braw])
            x1, bx1 = f32t.next()
            k.op("dve", lambda e, x1=x1, raw=raw, mu=mu: e.tensor_scalar(out=x1[:], in0=raw[:, 1:TT + 1], scalar1=ppc("om_" + mu), scalar2=None,
                                                                        op0=ALU.mult), reads=[braw, b_pp], writes=[bx1])
            k.op("dve", lambda e, x1=x1, raw=raw, mu=mu: e.scalar_tensor_tensor(out=x1[:], in0=raw[:, 0:TT], scalar=ppc("mu_" + mu), in1=x1[:],
                                                                               op0=ALU.mult, op1=ALU.add), reads=[braw, b_pp, bx1], writes=[bx1])
            xs[nm] = (x1, bx1)
        r_, br_ = xs["rr"]
        k_, bk_ = xs["rk"]
        v_, bv_ = xs["rv"]
        wa_, bwa_ = xs["rwa"]
        th, bth = bf16t.next()
        k.op("act", lambda e, th=th: e.activation(out=th[0:64, :], in_=wa_[0:64, :], func=AF.Tanh), reads=[bwa_], writes=[bth])
        k.op("dve", lambda e, th=th: e.tensor_copy(out=th[64:128, :], in_=wa_[64:128, :]), reads=[bwa_], writes=[bth])
        pz, bpz = psum.next()
        k.op("pe", lambda e, pz=pz, th=th: e.matmul(pz[:], lhsT=w2a2[0:64, :], rhs=th[0:64, :], start=True, stop=True),
             reads=[b_w2a2, bth], writes=[bpz])
        pa2, bpa2 = psum.next()
        k.op("pe", lambda e, pa2=pa2, th=th: e.matmul(pa2[:], lhsT=w2a2[64:128, :], rhs=th[64:128, :], start=True, stop=True),
             reads=[b_w2a2, bth], writes=[bpa2])
        sgm, bsgm, ds_sgm = next_of32()
        k.op("act", lambda e, sgm=sgm, pz=pz: e.activation(out=sgm[:], in_=pz[:], func=AF.Sigmoid, bias=ppc("w0")), reads=[bpz, b_pp], writes=[bsgm])
        store("sp", ds_sgm, d["RW"][5, :, tsl], sgm[:], bsgm)
        a_, ba_ = f32t.next()
        k.op("act", lambda e, a_=a_, pa2=pa2: e.activation(out=a_[:], in_=pa2[:], func=AF.Sigmoid, bias=ppc("a0")), reads=[bpa2, b_pp], writes=[ba_])
        kk0, bkk0 = f32t.next()
        k.op("dve", lambda e, kk0=kk0: e.tensor_scalar(out=kk0[:], in0=k_[:], scalar1=ppc("k_k"), scalar2=None, op0=ALU.mult),
             reads=[bk_, b_pp], writes=[bkk0])
        sqk, bsqk = bf16t.next()
        k.op("act", lambda e, sqk=sqk, kk0=kk0: e.activation(out=sqk[:], in_=kk0[:], func=AF.Square), reads=[bkk0], writes=[bsqk])
        pn, bpn = psum.next()
        k.op("pe", lambda e, pn=pn, sqk=sqk: e.matmul(pn[:], lhsT=bones_bf[:], rhs=sqk[:], start=True, stop=True), reads=[b_bones, bsqk], writes=[bpn])
        rn, brn = f32t.next()
        k.op("dve", lambda e, rn=rn, pn=pn: e.tensor_scalar(out=rn[:], in0=pn[:], scalar1=1e-24, scalar2=None, op0=ALU.max), reads=[bpn], writes=[brn])
        k.op("act", lambda e, rn=rn: e.activation(out=rn[:], in_=rn[:], func=AF.Ln), reads=[brn], writes=[brn])
        k.op("act", lambda e, rn=rn: e.activation(out=rn[:], in_=rn[:], func=AF.Exp, scale=-0.5), reads=[brn], writes=[brn])
        kk, bkk, ds_kk = next_of32()
        k.op("pool", lambda e, kk=kk, kk0=kk0, rn=rn: e.tensor_tensor(out=kk[:], in0=kk0[:], in1=rn[:], op=ALU.mult), reads=[bkk0, brn], writes=[bkk])
        store("sp", ds_kk, d["RW"][3, :, tsl], kk[:], bkk)
        bb, bbb, ds_bb = next_of32()
        k.op("pool", lambda e, bb=bb, kk=kk, a_=a_: e.tensor_tensor(out=bb[:], in0=kk[:], in1=a_[:], op=ALU.mult), reads=[bkk, ba_], writes=[bbb])
        store("sp", ds_bb, d["RW"][4, :, tsl], bb[:], bbb)
        f_, bf_ = f32t.next()
        k.op("dve", lambda e, f_=f_, a_=a_: e.tensor_scalar(out=f_[:], in0=a_[:], scalar1=ppc("k_a"), scalar2=ppc("omk_a"), op0=ALU.mult, op1=ALU.add),
             reads=[ba_, b_pp], writes=[bf_])
        kp, bkp, ds_kp = next_of32()
        k.op("dve", lambda e, kp=kp, f_=f_: e.tensor_tensor(out=kp[:], in0=k_[:], in1=f_[:], op=ALU.mult), reads=[bk_, bf_], writes=[bkp])
        store("sp", ds_kp, d["RW"][1, :, tsl], kp[:], bkp)
        ro, bro, ds_ro = next_of32()
        k.op("pool", lambda e, ro=ro: e.tensor_copy(out=ro[:], in_=r_[:]), reads=[br_], writes=[bro])
        store("sp", ds_ro, d["RW"][0, :, tsl], ro[:], bro)
        vo, bvo, ds_vo = next_of32()
        k.op("pool", lambda e, vo=vo: e.tensor_copy(out=vo[:], in_=v_[:]), reads=[bv_], writes=[bvo])
        store("sp", ds_vo, d["RW"][2, :, tsl], vo[:], bvo)
        rk, brk = bf16t.next()
        k.op("dve", lambda e, rk=rk, kp=kp: e.scalar_tensor_tensor(out=rk[:], in0=r_[:], scalar=ppc("r_k"), in1=kp[:], op0=ALU.mult, op1=ALU.mult),
             reads=[br_, bkp, b_pp], writes=[brk])
        pbn, bpbn = psum.next()
        k.op("pe", lambda e, pbn=pbn, rk=rk: e.matmul(pbn[:], lhsT=bones_bf[:], rhs=rk[:], start=True, stop=True), reads=[b_bones, brk], writes=[bpbn])
        bo, bbo, ds_bo = next_of32()
        k.op("dve", lambda e, bo=bo, pbn=pbn: e.tensor_tensor(out=bo[:], in0=pbn[:], in1=v_[:], op=ALU.mult), reads=[bpbn, bv_], writes=[bbo])
        store("sp", ds_bo, d["RW"][6, :, tsl], bo[:], bbo)
        pt, bp = mm_block(i, "rg", xt, bxt)
        so, bso, ds_so = next_of32()
        k.op("act", lambda e, so=so, pt=pt: e.activation(out=so[:], in_=pt[:], func=AF.Silu), reads=[bp], writes=[bso])
        store("sp", ds_so, d["RW"][7, :, tsl], so[:], bso)


def p1_dram(nc, S, kind_in="ExternalInput", kind_out="ExternalOutput"):
    d = {}
    d["xT"] = nc.dram_tensor("xT", [D_MODEL, S], BF16, kind=kind_in).ap()
    d["w1"] = nc.dram_tensor("w1", [D_MODEL, P1_NC], F32, kind="ExternalInput").ap()
    d["pp"] = nc.dram_tensor("pp", [128, NPP_IN], F32, kind="ExternalInput").ap()
    d["wuq"] = nc.dram_tensor("wuq", [512, 512], F32, kind="ExternalInput").ap()
    d["wukv"] = nc.dram_tensor("wukv", [256, 512], F32, kind="ExternalInput").ap()
    d["w2a2"] = nc.dram_tensor("w2a2", [128, 128], F32, kind="ExternalInput").ap()
    d["pos"] = nc.dram_tensor("pos", [1, S], I32, kind="ExternalInput").ap()
    d["ycT"] = nc.dram_tensor("ycT", [128, S], BF16, kind=kind_out).ap()
    d["QnT"] = nc.dram_tensor("QnT", [2, 128, S], BF16, kind=kind_out).ap()
    d["QpT"] = nc.dram_tensor("QpT", [2, 64, S], BF16, kind=kind_out).ap()
    d["KnT"] = nc.dram_tensor("KnT", [2, 128, S], BF16, kind=kind_out).ap()
    d["KpT"] = nc.dram_tensor("KpT", [64, S], BF16, kind=kind_out).ap()
    d["V"] = nc.dram_tensor("V", [2, 128, S // 128, 128], BF16, kind=kind_out).ap()
    d["SgM"] = nc.dram_tensor("SgM", [2, 128, S], F32, kind=kind_out).ap()
    d["RW"] = nc.dram_tensor("RW", [8, 128, S], F32, kind=kind_out).ap()
    return d


IN_SIZES = (512, 512, 512, 512, 512, 256, 64, 1024, 1664, 512)
IN_OFFS = np.concatenate([[0], np.cumsum(IN_SIZES)]).astype(int)
INV_FREQ = (10000.0 ** (-np.arange(0, 64, 2, dtype=np.float32) / 64)).astype(np.float32)


def host_layer_params(inp, l, g):
    w_in = inp["w_in"][l]
    o = IN_OFFS
    cols = []
    for j in range(4):
        cols.append(np.arange(o[j] + 128 * g, o[j] + 128 * g + 128))
    cols.append(np.arange(o[4], o[4] + 512))
    cols.append(np.arange(o[5], o[5] + 256))
    kpe = np.arange(o[6], o[6] + 64)
    cols.append(kpe)
    cols.append(np.concatenate([kpe[32:], kpe[:32]]))
    cols.append(np.arange(o[7] + 256 * g, o[7] + 256 * g + 256))
    rb = o[8]
    cols.append(np.arange(rb + 128 * g, rb + 128 * g + 128))
    cols.append(np.arange(rb + 576 + 128 * g, rb + 576 + 128 * g + 128))
    cols.append(np.arange(rb + 1088 + 128 * g, rb + 1088 + 128 * g + 128))
    cols.append(np.arange(rb + 512, rb + 576))
    cols.append(np.arange(rb + 1600, rb + 1664))
    cols.append(np.arange(o[9] + 128 * g, o[9] + 128 * g + 128))
    cols = np.concatenate(cols)
    assert cols.shape[0] == P1_NC
    w1 = np.ascontiguousarray(w_in[:, cols])
    pp = np.zeros((128, NPP_IN), np.float32)
    sl = slice(128 * g, 128 * g + 128)
    for j in range(3):
        pp[:, PP["cw%d" % j]] = inp["conv_w"][l, j, sl]
    for c in range(4):
        pp[:, PP["qg%d" % c]] = inp["q_norm_g"][l, c * 128:(c + 1) * 128]
    for c in range(2):
        pp[:, PP["kvg%d" % c]] = inp["kv_norm_g"][l, c * 128:(c + 1) * 128]
    mu = inp["rwkv_mu"][l]
    pp[:, PP["mu_r"]] = mu[0:512][sl]
    pp[:, PP["mu_k"]] = mu[576:1088][sl]
    pp[:, PP["mu_v"]] = mu[1088:1600][sl]
    pp[0:64, PP["mu_wa"]] = mu[512:576]
    pp[64:128, PP["mu_wa"]] = mu[1600:1664]
    pp[:, PP["w0"]] = inp["rwkv_w0"][l, sl]
    pp[:, PP["a0"]] = inp["rwkv_a0"][l, sl]
    pp[:, PP["k_k"]] = inp["rwkv_k_k"][l, sl]
    pp[:, PP["k_a"]] = inp["rwkv_k_a"][l, sl]
    pp[:, PP["r_k"]] = inp["rwkv_r_k"][l].reshape(-1)[sl]
    pp[:, PP["gn_g"]] = inp["rwkv_gn_g"][l, sl]
    pp[:, PP["gn_b"]] = inp["rwkv_gn_b"][l, sl]
    pp[0:32, PP["invf"]] = INV_FREQ
    pp[32:64, PP["invf"]] = INV_FREQ
    wq = inp["w_uq"][l]
    qc = []
    for h in (2 * g, 2 * g + 1):
        b0 = h * 192
        pe = np.arange(b0 + 128, b0 + 192)
        qc += [np.arange(b0, b0 + 128), pe, np.concatenate([pe[32:], pe[:32]])]
    wuq = np.ascontiguousarray(wq[:, np.concatenate(qc)])
    wkv = inp["w_ukv"][l]
    kc = []
    for h in (2 * g, 2 * g + 1):
        kc.append(np.arange(h * 256, h * 256 + 128))
    for h in (2 * g, 2 * g + 1):
        kc.append(np.arange(h * 256 + 128, h * 256 + 256))
    wukv = np.ascontiguousarray(wkv[:, np.concatenate(kc)])
    w2a2 = np.ascontiguousarray(np.concatenate([inp["rwkv_w2"][l][:, sl], inp["rwkv_a2"][l][:, sl]], axis=0))
    return {"w1": w1, "pp": pp, "wuq": wuq, "wukv": wukv, "w2a2": w2a2}


def build_p3(k, S, d, QC=512):
    nc = k.nc
    NQ = S // QC
    NB = S // 128
    ones_bf = k.sb([128, 128], BF16, "ones")
    b_ones = k.buf("ones")
    k.op("dve", lambda e: e.memset(ones_bf[:], 1.0), writes=[b_ones])
    masks = []
    for m in range(QC // 128):
        mi = k.sb([128, QC], I32, "mi")
        bmi = k.buf("mi")
        k.op("pool", lambda e, mi=mi, m=m: e.iota(mi[:], pattern=[[1, QC]], base=-128 * m, channel_multiplier=-1), writes=[bmi])
        mb = k.sb([128, QC], BF16, "mb")
        bmb = k.buf("mb")
        k.op("dve", lambda e, mi=mi, mb=mb: e.tensor_scalar(out=mb[:], in0=mi[:], scalar1=0.0, scalar2=None, op0=ALU.is_ge),
             reads=[bmi], writes=[bmb])
        masks.append((mb, bmb))
    KnT = k.sb([128, S], BF16, "KnT")
    bKn = k.buf("KnT")
    dsKn = k.dsem("kn")
    KpT = k.sb([64, S], BF16, "KpT")
    bKp = k.buf("KpT")
    dsKp = k.dsem("kp")
    Vt = k.sb([128, NB, 128], BF16, "V")
    bV = k.buf("V")
    dsV = k.dsem("v")
    NSPL = max(1, S // 2048)
    CS = S // NSPL
    for c in range(NSPL):
        k.dma("sp", dsKp, lambda e, c=c: e.dma_start(out=KpT[:, c * CS:(c + 1) * CS], in_=d["KpT"][:, c * CS:(c + 1) * CS]), writes=[bKp])
    qin = [(k.sb([128, QC], BF16, "qn"), k.sb([64, QC], BF16, "qp"), k.sb([128, QC], F32, "sg"), k.buf("qin"), k.dsem("qin")) for _ in range(2)]
    pS = Rot([(k.ps([128, QC], F32, "pS"), k.pbuf("pS")) for _ in range(3)])
    pO = Rot([(k.ps([128, QC], F32, "pO"), k.pbuf("pO")) for _ in range(2)])
    pL = Rot([(k.ps([128, QC], F32, "pL"), k.pbuf("pL")) for _ in range(2)])
    Pt = sb_rot(k, 4, [128, QC], BF16, "P")
    rc_rot = sb_rot(k, 2, [128, QC], F32, "rc")
    o_rot = sb_rot(k, 2, [128, QC], F32, "o")
    out_rot = [(k.sb([128, QC], BF16, "yo"), k.buf("yo"), k.dsem("yo")) for _ in range(2)]
    it = 0
    for h in range(2):
        for c in range(NSPL):
            k.dma("sp", dsKn, lambda e, c=c, h=h: e.dma_start(out=KnT[:, c * CS:(c + 1) * CS], in_=d["KnT"][h, :, c * CS:(c + 1) * CS]), writes=[bKn])
            nb = CS // 128
            k.dma("sp", dsV, lambda e, c=c, h=h, nb=nb: e.dma_start(out=Vt[:, c * nb:(c + 1) * nb, :], in_=d["V"][h, :, c * nb:(c + 1) * nb, :]), writes=[bV])

        def load_q(j, slot):
            qn, qp, sg, bq, dsq = qin[slot]
            sl = slice(j * QC, (j + 1) * QC)
            k.dma("sp", dsq, lambda e: e.dma_start(out=qn[:], in_=d["QnT"][h, :, sl]), writes=[bq])
            k.dma("sp", dsq, lambda e: e.dma_start(out=qp[:], in_=d["QpT"][h, :, sl]), writes=[bq])
            k.dma("sp", dsq, lambda e: e.dma_start(out=sg[:], in_=d["SgM"][h, :, sl]), writes=[bq])

        load_q(0, it % 2)
        for j in range(NQ):
            slot = it % 2
            it += 1
            if j + 1 < NQ:
                load_q(j + 1, it % 2)
            qn, qp, sg, bq, dsq = qin[slot]
            nkb = (j + 1) * (QC // 128)
            po, bpo = pO.next()
            pl, bpl = pL.next()
            sbanks = {}

            def emit_qk(kb):
                ps_, bps = pS.next()
                sbanks[kb] = (ps_, bps)
                ksl = slice(kb * 128, (kb + 1) * 128)
                k.op("pe", lambda e: e.matmul(ps_[:], lhsT=KnT[:, ksl], rhs=qn[:], start=True, stop=False), reads=[bKn, bq], writes=[bps])
                k.op("pe", lambda e: e.matmul(ps_[:], lhsT=KpT[:, ksl], rhs=qp[:], start=False, stop=True), reads=[bKp, bq], writes=[bps])

            emit_qk(0)
            for kb in range(nkb):
                if kb + 1 < nkb:
                    emit_qk(kb + 1)
                ps_, bps = sbanks.pop(kb)
                P, bP = Pt.next()
                k.op("act", lambda e, P=P, ps_=ps_: e.activation(out=P[:], in_=ps_[:], func=AF.Exp), reads=[bps], writes=[bP])
                m = kb - j * (QC // 128)
                if m >= 0:
                    mb, bmb = masks[m]
                    k.op("dve", lambda e, P=P, mb=mb: e.tensor_tensor(out=P[:], in0=P[:], in1=mb[:], op=ALU.mult), reads=[bP, bmb], writes=[bP])
                k.op("pe", lambda e, P=P, kb=kb: e.matmul(po[:], lhsT=Vt[:, kb, :], rhs=P[:], start=(kb == 0), stop=(kb == nkb - 1)),
                     reads=[bV, bP], writes=[bpo])
                k.op("pe", lambda e, P=P, kb=kb: e.matmul(pl[:], lhsT=ones_bf[:], rhs=P[:], start=(kb == 0), stop=(kb == nkb - 1)),
                     reads=[b_ones, bP], writes=[bpl])
            rc, brc = rc_rot.next()
            k.op("dve", lambda e, rc=rc: e.reciprocal(out=rc[:], in_=pl[:]), reads=[bpl], writes=[brc])
            o, bo = o_rot.next()
            k.op("dve", lambda e, o=o, rc=rc: e.tensor_tensor(out=o[:], in0=po[:], in1=rc[:], op=ALU.mult), reads=[bpo, brc], writes=[bo])
            yo, byo, dsy = out_rot[j % 2]
            k.op("pool", lambda e, o=o, yo=yo: e.tensor_tensor(out=yo[:], in0=o[:], in1=sg[:], op=ALU.mult), reads=[bo, bq], writes=[byo])
            k.dma("sp", dsy, lambda e, yo=yo, j=j, h=h: e.dma_start(out=d["ymT"][h, :, j * QC:(j + 1) * QC], in_=yo[:]), reads=[byo], is_output=True)


def p3_dram(nc, S, kind_in="ExternalInput", kind_out="ExternalOutput"):
    d = {}
    d["QnT"] = nc.dram_tensor("QnT", [2, 128, S], BF16, kind=kind_in).ap()
    d["QpT"] = nc.dram_tensor("QpT", [2, 64, S], BF16, kind=kind_in).ap()
    d["KnT"] = nc.dram_tensor("KnT", [2, 128, S], BF16, kind=kind_in).ap()
    d["KpT"] = nc.dram_tensor("KpT", [64, S], BF16, kind=kind_in).ap()
    d["V"] = nc.dram_tensor("V", [2, 128, S // 128, 128], BF16, kind=kind_in).ap()
    d["SgM"] = nc.dram_tensor("SgM", [2, 128, S], F32, kind=kind_in).ap()
    d["ymT"] = nc.dram_tensor("ymT", [2, 128, S], BF16, kind=kind_out).ap()
    return d


def build_p4(k, S, d, TT=512, L=128):
    nc = k.nc
    NT = S // TT
    NCH = TT // L
    pp = k.sb([128, NPP_IN], F32, "pp4")
    b_pp = k.buf("pp4")
    ds_c = k.dsem("c4")
    k.dma("sp", ds_c, lambda e: e.dma_start(out=pp[:], in_=d["pp"][:, :]), writes=[b_pp])

    def ppc(name):
        return pp[:, PP[name]:PP[name] + 1]

    mi = k.sb([128, 128], I32, "mi4")
    bmi = k.buf("mi4")
    k.op("pool", lambda e: e.iota(mi[:], pattern=[[1, 128]], base=0, channel_multiplier=-1), writes=[bmi])
    m_low = k.sb([128, 128], F32, "mlow"); m_up = k.sb([128, 128], F32, "mup"); m_upi = k.sb([128, 128], F32, "mupi")
    ident_bf = k.sb([128, 128], BF16, "identb"); ident_f = k.sb([128, 128], F32, "identf")
    b_m = k.buf("masks")
    k.op("dve", lambda e: e.tensor_scalar(out=m_low[:], in0=mi[:], scalar1=0.0, scalar2=None, op0=ALU.is_lt), reads=[bmi], writes=[b_m])
    k.op("dve", lambda e: e.tensor_scalar(out=m_up[:], in0=mi[:], scalar1=0.0, scalar2=None, op0=ALU.is_gt), reads=[bmi], writes=[b_m])
    k.op("dve", lambda e: e.tensor_scalar(out=m_upi[:], in0=mi[:], scalar1=0.0, scalar2=None, op0=ALU.is_ge), reads=[bmi], writes=[b_m])
    k.op("dve", lambda e: e.tensor_scalar(out=ident_bf[:], in0=mi[:], scalar1=0.0, scalar2=None, op0=ALU.is_equal), reads=[bmi], writes=[b_m])
    k.op("dve", lambda e: e.tensor_scalar(out=ident_f[:], in0=mi[:], scalar1=0.0, scalar2=None, op0=ALU.is_equal), reads=[bmi], writes=[b_m])
    rmask = k.sb([128, TT], F32, "rmask")
    b_rm = k.buf("rmask")
    k.op("dve", lambda e: e.memset(rmask[:], 1.0), writes=[b_rm])
    for c in range(NCH):
        k.op("dve", lambda e, c=c: e.memset(rmask[:, c * L:c * L + 1], 0.0), writes=[b_rm])
    eps_t = k.sb([128, 1], F32, "eps4")
    b_eps = k.buf("eps4")
    k.op("dve", lambda e: e.memset(eps_t[:], 64e-5), writes=[b_eps])
    H = k.sb([128, 64], F32, "H"); Hbf = k.sb([128, 64], BF16, "Hbf"); HE = k.sb([128, 64], F32, "HE")
    bH = [k.buf("H0"), k.buf("H1")]
    bHE = [k.buf("HE0"), k.buf("HE1")]
    k.op("dve", lambda e: e.memset(H[:], 0.0), writes=bH)
    k.op("dve", lambda e: e.memset(Hbf[:], 0.0), writes=bH)

    NAMES = ["r", "kp", "v", "kk", "b", "sgm", "bonusv", "sg_r"]
    inrot = [({n: k.sb([128, TT], F32, "in_" + n) for n in NAMES}, k.buf("in"), k.dsem("in4")) for _ in range(2)]
    f32t = sb_rot(k, 6, [128, TT], F32, "g")
    bft = sb_rot(k, 10, [128, TT], BF16, "gb")
    tokt = sb_rot(k, 8, [128, 128], BF16, "tok")
    mat = sb_rot(k, 24, [128, 128], BF16, "mat")
    smallb = sb_rot(k, 6, [128, 64], BF16, "sm")
    smallf = sb_rot(k, 6, [128, 64], F32, "smf")
    stat = sb_rot(k, 8, [128, 8], F32, "stat")
    psA = Rot([(k.ps([128, 512], F32, "psA")[:, 0:128], k.pbuf("psA")) for _ in range(4)])
    psB = Rot([(k.ps([128, 512], F32, "psB")[:, 0:128], k.pbuf("psB")) for _ in range(2)])
    psT = Rot([(k.ps([128, 1024], BF16, "psT")[:, 0:128], k.pbuf("psT")) for _ in range(2)])
    outt = [(k.sb([128, TT], BF16, "yo4"), k.buf("yo4"), k.dsem("yo4")) for _ in range(2)]
    ytile = sb_rot(k, 2, [128, TT], F32, "yt4")
    ynrot = sb_rot(k, 2, [128, 128], F32, "ynb")

    def load(i):
        tl, bt, ds = inrot[i % 2]
        for j, n in enumerate(NAMES):
            k.dma("sp", ds, lambda e, t=tl[n], j=j, i=i: e.dma_start(out=t[:], in_=d["RW"][j, :, i * TT:(i + 1) * TT]), writes=[bt])

    engs = ["dve", "pool"]
    load(0)
    for i in range(NT):
        if i + 1 < NT:
            load(i + 1)
        tl, bt, ds = inrot[i % 2]
        lw, blw = f32t.next()
        k.op("dve", lambda e: e.tensor_scalar(out=lw[:], in0=tl["sgm"][:], scalar1=-math.exp(-0.5), scalar2=None, op0=ALU.mult), reads=[bt], writes=[blw])
        cs, bcs = f32t.next()
        k.op("dve", lambda e: e.tensor_tensor_scan(out=cs[:], data0=rmask[:], data1=lw[:], initial=0.0, op0=ALU.mult, op1=ALU.add),
             reads=[b_rm, blw], writes=[bcs])
        Ep, bEp = f32t.next()
        k.op("act", lambda e: e.activation(out=Ep[:], in_=cs[:], func=AF.Exp), reads=[bcs], writes=[bEp])
        En, bEn = f32t.next()
        k.op("act", lambda e: e.activation(out=En[:], in_=cs[:], func=AF.Exp, scale=-1.0), reads=[bcs], writes=[bEn])
        k.op("pool", lambda e: e.tensor_tensor(out=lw[:], in0=cs[:], in1=lw[:], op=ALU.subtract), reads=[bcs, blw], writes=[blw])
        Epv, bEpv = f32t.next()
        k.op("act", lambda e: e.activation(out=Epv[:], in_=lw[:], func=AF.Exp), reads=[blw], writes=[bEpv])
        rt, brt = bft.next()
        k.op("dve", lambda e: e.tensor_tensor(out=rt[:], in0=tl["r"][:], in1=Ep[:], op=ALU.mult), reads=[bt, bEp], writes=[brt])
        kt, bkt = bft.next()
        k.op("pool", lambda e: e.tensor_tensor(out=kt[:], in0=tl["kp"][:], in1=En[:], op=ALU.mult), reads=[bt, bEn], writes=[bkt])
        btl, bbt = bft.next()
        k.op("dve", lambda e: e.tensor_tensor(out=btl[:], in0=tl["b"][:], in1=En[:], op=ALU.mult), reads=[bt, bEn], writes=[bbt])
        at, bat = bft.next()
        k.op("dve", lambda e: e.scalar_tensor_tensor(out=at[:], in0=tl["kk"][:], scalar=-1.0, in1=Epv[:], op0=ALU.mult, op1=ALU.mult),
             reads=[bt, bEpv], writes=[bat])
        vb, bvb = bft.next()
        k.op("pool", lambda e: e.tensor_copy(out=vb[:], in_=tl["v"][:]), reads=[bt], writes=[bvb])
        yt, byt = ytile.next()
        for c in range(NCH):
            csl = slice(c * L, (c + 1) * L)
            toks = []
            for src, bsrc in ((kt, bkt), (btl, bbt), (vb, bvb)):
                pt, bpt = psT.next()
                k.op("pe", lambda e: e.transpose(pt[:], src[:, csl], ident_bf[:]), reads=[bsrc, b_m], writes=[bpt])
                tk, btk = tokt.next()
                k.op("act", lambda e: e.activation(out=tk[:], in_=pt[:], func=AF.Copy), reads=[bpt], writes=[btk])
                toks.append((tk, btk))
            (ktT, bktT), (btT, bbtT), (vT, bvT) = toks
            ynb, bynb = ynrot.next()
            for h in range(2):
                cp = slice(64 * h, 64 * h + 64)
                ei = [0]

                def eng():
                    ei[0] += 1
                    return engs[ei[0] % 2]

                def amat(l_, bl, r_, br, mask):
                    pa, bpa = psA.next()
                    k.op("pe", lambda e: e.matmul(pa[:], lhsT=l_[cp, csl], rhs=r_[cp, csl], start=True, stop=True), reads=[bl, br], writes=[bpa])
                    m, bm = mat.next()
                    k.op("dve", lambda e: e.tensor_tensor(out=m[:], in0=pa[:], in1=mask[:], op=ALU.mult), reads=[bpa, b_m], writes=[bm])
                    return m, bm
                P, bP = amat(at, bat, btl, bbt, m_low)
                PT, bPT = amat(btl, bbt, at, bat, m_up)
                AakT, bAak = amat(kt, bkt, at, bat, m_up)
                ArbT, bArb = amat(btl, bbt, rt, brt, m_upi)
                ArkT, bArk = amat(kt, bkt, rt, brt, m_upi)
                TTm, bTT = mat.next()
                k.op(eng(), lambda e: e.tensor_tensor(out=TTm[:], in0=PT[:], in1=ident_f[:], op=ALU.add), reads=[bPT, b_m], writes=[bTT])
                nit = 6
                for it_ in range(nit):
                    pa, bpa = psA.next()
                    k.op("pe", lambda e: e.matmul(pa[:], lhsT=PT[:], rhs=P[:], start=True, stop=True), reads=[bPT, bP], writes=[bpa])
                    P2, bP2 = mat.next()
                    k.op("act", lambda e: e.activation(out=P2[:], in_=pa[:], func=AF.Copy), reads=[bpa], writes=[bP2])
                    if it_ < nit - 1:
                        pb_, bpb_ = psA.next()
                        k.op("pe", lambda e: e.matmul(pb_[:], lhsT=P[:], rhs=PT[:], start=True, stop=True), reads=[bPT, bP], writes=[bpb_])
                        PT2, bPT2 = mat.next()
                        k.op("act", lambda e: e.activation(out=PT2[:], in_=pb_[:], func=AF.Copy), reads=[bpb_], writes=[bPT2])
                    pc_, bpc_ = psA.next()
                    k.op("pe", lambda e: e.matmul(pc_[:], lhsT=P2[:], rhs=TTm[:], start=True, stop=True), reads=[bP2, bTT], writes=[bpc_])
                    TT2, bTT2 = mat.next()
                    k.op("dve", lambda e: e.tensor_tensor(out=TT2[:], in0=pc_[:], in1=TTm[:], op=ALU.add), reads=[bpc_, bTT], writes=[bTT2])
                    P, bP = P2, bP2
                    if it_ < nit - 1:
                        PT, bPT = PT2, bPT2
                    TTm, bTT = TT2, bTT2
                vh = vT[:, 64 * h:64 * h + 64]
                E_L = Ep[:, c * L + L - 1:c * L + L]
                k.op("pool", lambda e: e.tensor_scalar(out=HE[cp, :], in0=H[cp, :], scalar1=E_L[cp, :], scalar2=None, op0=ALU.mult),
                     reads=[bH[h], bEp], writes=[bHE[h]])
                px, bpx = psB.next()
                k.op("pe", lambda e: e.matmul(px[:, 0:64], lhsT=AakT[:], rhs=vh, start=True, stop=False), reads=[bAak, bvT], writes=[bpx])
                k.op("pe", lambda e: e.matmul(px[:, 0:64], lhsT=at[cp, csl], rhs=Hbf[cp, :], start=False, stop=True), reads=[bat, bH[h]], writes=[bpx])
                Xb, bXb = smallb.next()
                k.op("act", lambda e: e.activation(out=Xb[:], in_=px[:, 0:64], func=AF.Copy), reads=[bpx], writes=[bXb])
                pu, bpu = psB.next()
                k.op("pe", lambda e: e.matmul(pu[:, 0:64], lhsT=TTm[:], rhs=Xb[:], start=True, stop=True), reads=[bTT, bXb], writes=[bpu])
                Ub, bUb = smallb.next()
                k.op("dve", lambda e: e.tensor_copy(out=Ub[:], in_=pu[:, 0:64]), reads=[bpu], writes=[bUb])
                py, bpy = psB.next()
                k.op("pe", lambda e: e.matmul(py[:, 0:64], lhsT=ArkT[:], rhs=vh, start=True, stop=False), reads=[bArk, bvT], writes=[bpy])
                k.op("pe", lambda e: e.matmul(py[:, 0:64], lhsT=rt[cp, csl], rhs=Hbf[cp, :], start=False, stop=False), reads=[brt, bH[h]], writes=[bpy])
                k.op("pe", lambda e: e.matmul(py[:, 0:64], lhsT=ArbT[:], rhs=Ub[:], start=False, stop=True), reads=[bArb, bUb], writes=[bpy])
                ph, bph = psB.next()
                k.op("pe", lambda e: e.matmul(ph[:, 0:64], lhsT=ktT[:], rhs=vh, start=True, stop=False), reads=[bktT, bvT], writes=[bph])
                k.op("pe", lambda e: e.matmul(ph[:, 0:64], lhsT=btT[:], rhs=Ub[:], start=False, stop=True), reads=[bbtT, bUb], writes=[bph])
                k.op("dve", lambda e: e.scalar_tensor_tensor(out=H[cp, :], in0=ph[cp, 0:64], scalar=E_L[cp, :], in1=HE[cp, :], op0=ALU.mult, op1=ALU.add),
                     reads=[bph, bEp, bHE[h]], writes=[bH[h]])
                k.op("act", lambda e: e.activation(out=Hbf[cp, :], in_=H[cp, :], func=AF.Copy), reads=[bH[h]], writes=[bH[h]])
                st, bst = stat.next()
                k.op("dve", lambda e: e.bn_stats(out=st[:, 0:6], in_=py[:, 0:64]), reads=[bpy], writes=[bst])
                mv, bmv = stat.next()
                k.op("dve", lambda e: e.bn_aggr(out=mv[:, 0:2], in_=st[:, 0:6]), reads=[bst], writes=[bmv])
                k.op("act", lambda e: e.activation(out=mv[:, 2:3], in_=mv[:, 1:2], func=AF.Ln, bias=eps_t[:, 0:1]), reads=[bmv, b_eps], writes=[bmv])
                k.op("act", lambda e: e.activation(out=mv[:, 2:3], in_=mv[:, 2:3], func=AF.Exp, scale=-0.5), reads=[bmv], writes=[bmv])
                k.op("dve", lambda e: e.tensor_scalar(out=ynb[:, 64 * h:64 * h + 64], in0=py[:, 0:64], scalar1=mv[:, 0:1], scalar2=mv[:, 2:3],
                                                      op0=ALU.subtract, op1=ALU.mult), reads=[bpy, bmv], writes=[bynb])
            pt2, bpt2 = psB.next()
            k.op("pe", lambda e: e.transpose(pt2[:], ynb[:], ident_f[:]), reads=[bynb, b_m], writes=[bpt2])
            k.op("act", lambda e: e.activation(out=yt[:, csl], in_=pt2[:], func=AF.Identity, scale=ppc("gn_g"), bias=ppc("gn_b")),
                 reads=[bpt2, b_pp], writes=[byt])
        k.op("dve", lambda e: e.tensor_tensor(out=yt[:], in0=yt[:], in1=tl["bonusv"][:], op=ALU.add), reads=[byt, bt], writes=[byt])
        yo, byo, dsy = outt[i % 2]
        k.op("pool", lambda e: e.tensor_tensor(out=yo[:], in0=yt[:], in1=tl["sg_r"][:], op=ALU.mult), reads=[byt, bt], writes=[byo])
        k.dma("sp", dsy, lambda e, i=i: e.dma_start(out=d["yrT"][:, i * TT:(i + 1) * TT], in_=yo[:]), reads=[byo], is_output=True)


def p4_dram(nc, S, kind_in="ExternalInput", kind_out="ExternalOutput"):
    d = {}
    d["RW"] = nc.dram_tensor("RW", [8, 128, S], F32, kind=kind_in).ap()
    d["pp"] = nc.dram_tensor("pp", [128, NPP_IN], F32, kind="ExternalInput").ap()
    d["yrT"] = nc.dram_tensor("yrT", [128, S], BF16, kind=kind_out).ap()
    return d


ALPHA = 4.0 ** 0.25


def build_p5(k, NTOK, d, only_transpose=False):
    nc = k.nc
    NBLK = NTOK // 128
    KC = D_MODEL // 128
    ident_f = k.sb([128, 128], F32, "identf5")
    mi = k.sb([128, 128], I32, "mi5")
    bmi = k.buf("mi5")
    b_id = k.buf("id5")
    k.op("pool", lambda e: e.iota(mi[:], pattern=[[1, 128]], base=0, channel_multiplier=-1), writes=[bmi])
    k.op("dve", lambda e: e.tensor_scalar(out=ident_f[:], in0=mi[:], scalar1=0.0, scalar2=None, op0=ALU.is_equal), reads=[bmi], writes=[b_id])
    if not only_transpose:
        eps_t = k.sb([128, 1], F32, "eps5")
        b_eps = k.buf("eps5")
        k.op("dve", lambda e: e.memset(eps_t[:], 1e-5), writes=[b_eps])
        wbf = k.sb([128, KC, D_MODEL], BF16, "wo")
        b_w = k.buf("wo")
        stg = [(k.sb([128, D_MODEL], F32, "wostg"), k.buf("wostg"), k.dsem("wost")) for _ in range(2)]
        for c in range(KC):
            st, bst, dss = stg[c % 2]
            k.dma("sp", dss, lambda e: e.dma_start(out=st[:], in_=d["w_out"][c * 128:(c + 1) * 128, :]), writes=[bst])
            k.op("dve" if c % 2 == 0 else "pool", lambda e: e.tensor_copy(out=wbf[:, c, :], in_=st[:]), reads=[bst], writes=[b_w])
        gbc = k.sb([128, D_MODEL], F32, "gbc")
        bbc = k.sb([128, D_MODEL], F32, "bbc")
        b_gb = k.buf("gb")
        ds_gb = k.dsem("gb")
        k.dma("sp", ds_gb, lambda e: e.dma_start(out=gbc[:], in_=d["ln_g"][0:1, :].broadcast_to([128, D_MODEL])), writes=[b_gb])
        k.dma("sp", ds_gb, lambda e: e.dma_start(out=bbc[:], in_=d["ln_b"][0:1, :].broadcast_to([128, D_MODEL])), writes=[b_gb])
        mixr = [(k.sb([128, KC, 128], BF16, "mx"), k.buf("mx"), k.dsem("mx")) for _ in range(2)]
        psO = [(k.ps([128, 512], F32, "psO"), k.pbuf("psO")) for _ in range(4)]
        statr = sb_rot(k, 4, [128, 32], F32, "st5")
    xr = [(k.sb([128, D_MODEL], F32, "x5"), k.buf("x5"), k.dsem("x5")) for _ in range(2)]
    yr = [(k.sb([128, D_MODEL], F32, "y5"), k.buf("y5"), k.dsem("y5")) for _ in range(2)]
    xTr = [(k.sb([128, KC, 128], BF16, "xT5"), k.buf("xT5"), k.dsem("xT5")) for _ in range(2)]
    psT = Rot([(k.ps([128, 512], F32, "psT5"), k.pbuf("psT5")) for _ in range(2)])

    def load(t):
        xt, bx, dsx = xr[t % 2]
        k.dma("sp", dsx, lambda e: e.dma_start(out=xt[:], in_=d["x"][t * 128:(t + 1) * 128, :]), writes=[bx])
        if not only_transpose:
            mx, bm, dsm = mixr[t % 2]
            k.dma("sp", dsm, lambda e: e.dma_start(out=mx[:], in_=d["mixT"][:, t * 128:(t + 1) * 128].rearrange("(c p) t -> p c t", p=128)), writes=[bm])

    load(0)
    for t in range(NBLK):
        if t + 1 < NBLK:
            load(t + 1)
        xt, bx, dsx = xr[t % 2]
        if only_transpose:
            y, by = xt, bx
        else:
            mx, bm, dsm = mixr[t % 2]
            y, by, dsy = yr[t % 2]
            st, bst = statr.next()
            for nb in range(4):
                po, bpo = psO[nb]
                for c in range(KC):
                    k.op("pe", lambda e: e.matmul(po[:], lhsT=mx[:, c, :], rhs=wbf[:, c, nb * 512:(nb + 1) * 512], start=(c == 0), stop=(c == KC - 1)),
                         reads=[bm, b_w], writes=[bpo])
                k.op("dve", lambda e: e.scalar_tensor_tensor(out=y[:, nb * 512:(nb + 1) * 512], in0=xt[:, nb * 512:(nb + 1) * 512], scalar=ALPHA, in1=po[:],
                                                             op0=ALU.mult, op1=ALU.add), reads=[bx, bpo], writes=[by])
                k.op("dve", lambda e: e.bn_stats(out=st[:, nb * 6:(nb + 1) * 6], in_=y[:, nb * 512:(nb + 1) * 512]), reads=[by], writes=[bst])
            k.op("dve", lambda e: e.bn_aggr(out=st[:, 24:26], in_=st[:, 0:24]), reads=[bst], writes=[bst])
            k.op("act", lambda e: e.activation(out=st[:, 26:27], in_=st[:, 25:26], func=AF.Ln, bias=eps_t[:, 0:1]), reads=[bst, b_eps], writes=[bst])
            k.op("act", lambda e: e.activation(out=st[:, 26:27], in_=st[:, 26:27], func=AF.Exp, scale=-0.5), reads=[bst], writes=[bst])
            k.op("dve", lambda e: e.tensor_scalar(out=y[:], in0=y[:], scalar1=st[:, 24:25], scalar2=st[:, 26:27], op0=ALU.subtract, op1=ALU.mult),
                 reads=[by, bst], writes=[by])
            k.op("pool", lambda e: e.tensor_tensor(out=y[:], in0=y[:], in1=gbc[:], op=ALU.mult), reads=[by, b_gb], writes=[by])
            k.op("dve", lambda e: e.tensor_tensor(out=y[:], in0=y[:], in1=bbc[:], op=ALU.add), reads=[by, b_gb], writes=[by])
            k.dma("sp", dsy, lambda e: e.dma_start(out=d["x1"][t * 128:(t + 1) * 128, :], in_=y[:]), reads=[by], is_output=True)
        xT, bxT, dsxT = xTr[t % 2]
        for q4 in range(KC // 4):
            pt, bpt = psT.next()
            for j in range(4):
                dc = q4 * 4 + j
                k.op("pe", lambda e: e.transpose(pt[:, j * 128:(j + 1) * 128], y[:, dc * 128:(dc + 1) * 128], ident_f[:]), reads=[by, b_id], writes=[bpt])
            k.op("act", lambda e: e.activation(out=xT[:, q4 * 4:(q4 + 1) * 4, :], in_=pt[:].rearrange("p (c t) -> p c t", c=4), func=AF.Copy),
                 reads=[bpt], writes=[bxT])
        k.dma("sp", dsxT, lambda e: e.dma_start(out=d["x1T"][:, t * 128:(t + 1) * 128].rearrange("(c p) t -> p c t", p=128), in_=xT[:]),
              reads=[bxT], is_output=True)


def p5_dram(nc, NTOK, only_transpose=False):
    d = {}
    d["x"] = nc.dram_tensor("x", [NTOK, D_MODEL], F32, kind="ExternalInput").ap()
    if not only_transpose:
        d["mixT"] = nc.dram_tensor("mixT", [D_MODEL, NTOK], BF16, kind="ExternalInput").ap()
        d["w_out"] = nc.dram_tensor("w_out", [D_MODEL, D_MODEL], F32, kind="ExternalInput").ap()
        d["ln_g"] = nc.dram_tensor("ln_g", [1, D_MODEL], F32, kind="ExternalInput").ap()
        d["ln_b"] = nc.dram_tensor("ln_b", [1, D_MODEL], F32, kind="ExternalInput").ap()
        d["x1"] = nc.dram_tensor("x1", [NTOK, D_MODEL], F32, kind="ExternalOutput").ap()
    d["x1T"] = nc.dram_tensor("x1T", [D_MODEL, NTOK], BF16, kind="ExternalOutput").ap()
    return d


_PROGS = {}


def _prog(name, S):
    key = (name, S)
    if key in _PROGS:
        return _PROGS[key]
    nc = bass.Bass("TRN2", target_bir_lowering=False)
    with contextlib.ExitStack() as es:
        k = KB(nc, es)
        if name == "p1":
            build_p1(k, S, p1_dram(nc, S))
        elif name == "p3":
            build_p3(k, S, p3_dram(nc, S))
        elif name == "p4":
            build_p4(k, S, p4_dram(nc, S))
        elif name == "p5":
            build_p5(k, S, p5_dram(nc, S))
        elif name == "p0":
            build_p5(k, S, p5_dram(nc, S, only_transpose=True), only_transpose=True)
        k.finish()
    _PROGS[key] = nc
    return nc


def _run(nc, in_maps):
    return run_bass_kernel_spmd(nc, in_maps, core_ids=list(range(NCORES))).results


def kernel(**inp):
    inp = {k_: np.asarray(v) for k_, v in inp.items()}
    x = np.ascontiguousarray(inp["x"], dtype=np.float32)
    B, S, D = x.shape
    NTOK = B * S // NCORES
    xtok = x.reshape(NCORES, NTOK, D)
    res = _run(_prog("p0", NTOK), [{"x": xtok[c]} for c in range(NCORES)])
    xT = [np.concatenate([np.asarray(res[4 * b + q]["x1T"]) for q in range(4)], axis=1) for b in range(B)]
    cur = xtok
    L = inp["w_in"].shape[0]
    for l in range(L):
        hps = [host_layer_params(inp, l, g) for g in range(4)]
        maps = []
        for c in range(NCORES):
            b, g = c // 4, c % 4
            m = dict(hps[g])
            m["xT"] = xT[b]
            m["pos"] = np.ascontiguousarray(inp["positions"][b:b + 1]).astype(np.int32)
            maps.append(m)
        r1 = _run(_prog("p1", S), maps)
        r3 = _run(_prog("p3", S), [{n: np.asarray(r1[c][n]) for n in ("QnT", "QpT", "KnT", "KpT", "V", "SgM")} for c in range(NCORES)])
        r4 = _run(_prog("p4", S), [{"RW": np.asarray(r1[c]["RW"]), "pp": hps[c % 4]["pp"]} for c in range(NCORES)])
        mixT = []
        for b in range(B):
            rows = [np.asarray(r1[4 * b + g]["ycT"]) for g in range(4)]
            for g in range(4):
                ym = np.asarray(r3[4 * b + g]["ymT"])
                rows += [ym[0], ym[1]]
            rows += [np.asarray(r4[4 * b + g]["yrT"]) for g in range(4)]
            mixT.append(np.concatenate(rows, axis=0))
        maps = []
        for c in range(NCORES):
            b, q = c // 4, c % 4
            maps.append({"x": np.ascontiguousarray(cur[c]), "mixT": np.ascontiguousarray(mixT[b][:, q * NTOK:(q + 1) * NTOK]),
                         "w_out": np.ascontiguousarray(inp["w_out"][l]), "ln_g": np.ascontiguousarray(inp["ln_g"][l][None, :]),
                         "ln_b": np.ascontiguousarray(inp["ln_b"][l][None, :])})
        r5 = _run(_prog("p5", NTOK), maps)
        cur = np.stack([np.asarray(r5[c]["x1"]) for c in range(NCORES)], axis=0)
        xT = [np.concatenate([np.asarray(r5[4 * b + q]["x1T"]) for q in range(4)], axis=1) for b in range(B)]
    return cur.reshape(B, S, D).astype(np.float32)
```

```python
import contextlib
import math
import types
import numpy as np
import ml_dtypes
import concourse.bass as bass
import concourse.mybir as mybir
from concourse.bass_utils import run_bass_kernel_spmd

F32 = mybir.dt.float32
BF16 = mybir.dt.bfloat16
I32 = mybir.dt.int32
AF = mybir.ActivationFunctionType
ALU = mybir.AluOpType

D_MODEL = 2048
NCORES = 8
import os
_DBG_LEVEL = int(os.environ.get('DBG_LEVEL', '3'))
ENGS = ("pe", "act", "dve", "pool", "sp")


def _snap(fn):
    if fn.__closure__ is None:
        return fn
    cells = []
    for c in fn.__closure__:
        try:
            cells.append(types.CellType(c.cell_contents))
        except ValueError:
            cells.append(c)
    return types.FunctionType(fn.__code__, fn.__globals__, fn.__name__, fn.__defaults__, tuple(cells))


class Buf:
    __slots__ = ("name", "w", "r", "excl")

    def __init__(self, name, excl=False):
        self.name = name
        self.w = None
        self.r = {}
        self.excl = excl


class KB:
    _guid = 0

    def __init__(self, nc, es):
        self.nc = nc
        self.es = es
        self.rec = {e: [] for e in ENGS}
        self.cnt = {e: 0 for e in ENGS}
        self.seen = {e: {} for e in ENGS}
        self.sems = {}
        self.dcnt = {}
        self.nsem = 0
        for e in ENGS:
            if e != "sp":
                self.sems[e] = es.enter_context(nc.semaphore(self.uid("s_" + e)))
        self.out_events = []

    def uid(self, p):
        KB._guid += 1
        return "%s%d" % (p, KB._guid)

    def buf(self, name="b"):
        return Buf(name)

    def pbuf(self, name="p"):
        return Buf(name, excl=True)

    def dsem(self, name):
        key = ("d", self.uid(name))
        self.sems[key] = self.es.enter_context(self.nc.semaphore(key[1]))
        self.dcnt[key] = 0
        return key

    def sb(self, shape, dtype, name="t"):
        return self.es.enter_context(self.nc.sbuf_tensor(self.uid(name), list(shape), dtype))

    def ps(self, shape, dtype=F32, name="p"):
        return self.es.enter_context(self.nc.psum_tensor(self.uid(name), list(shape), dtype))

    def _deps(self, eng, reads, writes):
        evs = {}

        def add(ev, kind):
            if ev is None:
                return
            key, val = ev
            if key == eng and eng == "pe":
                return
            if evs.get(key, 0) < val:
                evs[key] = val

        for b in reads:
            add(b.w, "raw")
            if b.excl:
                for key, val in b.r.items():
                    if key != eng:
                        add((key, val), "rar")
        for b in writes:
            add(b.w, "waw")
            for key, val in b.r.items():
                add((key, val), "war")
        waits = []
        seen = self.seen[eng]
        for key, val in evs.items():
            if seen.get(key, 0) >= val:
                continue
            seen[key] = val
            waits.append((self.sems[key], val))
        return waits

    def _post(self, ev, reads, writes):
        key, val = ev
        for b in reads:
            if b.r.get(key, 0) < val:
                b.r[key] = val
        for b in writes:
            b.w = ev
            b.r = {}

    def op(self, eng, fn, reads=(), writes=()):
        fn = _snap(fn)
        waits = self._deps(eng, reads, writes)
        self.cnt[eng] += 1
        idx = self.cnt[eng]
        sem = self.sems[eng]

        def run(e, waits=waits, fn=fn, sem=sem):
            for s, v in waits:
                e.wait_ge(s, v)
            fn(e).then_inc(sem, 1)

        self.rec[eng].append(run)
        ev = (eng, idx)
        self._post(ev, reads, writes)
        return ev

    def dma(self, q, dkey, fn, reads=(), writes=(), is_output=False):
        fn = _snap(fn)
        waits = self._deps(q, reads, writes)
        self.dcnt[dkey] += 16
        val = self.dcnt[dkey]
        sem = self.sems[dkey]

        def run(e, waits=waits, fn=fn, sem=sem):
            for s, v in waits:
                e.wait_ge(s, v)
            fn(e).then_inc(sem, 16)

        self.rec[q].append(run)
        ev = (dkey, val)
        self._post(ev, reads, writes)
        if is_output:
            self.out_events.append(ev)
        return ev

    def cc(self, kind, groups, in_ap, out_ap, reads=(), writes=()):
        dkey = self.dsem("cc")
        waits = self._deps("pool", reads, writes)
        sem = self.sems[dkey]

        def run(e, waits=waits, sem=sem):
            for s_, v in waits:
                e.wait_ge(s_, v)
            e.collective_compute(kind, ALU.bypass, replica_groups=groups, ins=[in_ap], outs=[out_ap]).then_inc(sem, 1)

        self.rec["pool"].append(run)
        self.dcnt[dkey] = 1
        ev = (dkey, 1)
        self._post(ev, reads, writes)
        self.out_events.append(ev)
        return ev

    def finish(self):
        last = {}
        for key, val in self.out_events:
            if last.get(key, 0) < val:
                last[key] = val
        fin = [(self.sems[k], v) for k, v in last.items()]

        def run(e, fin=fin):
            for s, v in fin:
                e.wait_ge(s, v)

        self.rec["sp"].append(run)
        rec = self.rec
        with self.nc.Block() as block:
            @block.sync
            def _(e):
                for r in rec["sp"]:
                    r(e)

            @block.tensor
            def _(e):
                for r in rec["pe"]:
                    r(e)

            @block.scalar
            def _(e):
                for r in rec["act"]:
                    r(e)

            @block.vector
            def _(e):
                for r in rec["dve"]:
                    r(e)

            @block.gpsimd
            def _(e):
                for r in rec["pool"]:
                    r(e)


def mix_dst(d, key, base, tok0, n, h=None):
    if d.get("mix_q") is not None:
        msq, ntok = d["mix_q"]
        qt, off = tok0 // ntok, tok0 % ntok
        return msq[qt * 512 + base:qt * 512 + base + 128, off:off + n]
    if h is None:
        return d[key][:, tok0:tok0 + n]
    return d[key][h, :, tok0:tok0 + n]


class Rot:
    def __init__(self, items):
        self.items = items
        self.i = 0

    def next(self):
        it = self.items[self.i % len(self.items)]
        self.i += 1
        return it


def sb_rot(k, n, shape, dtype, name):
    return Rot([(k.sb(shape, dtype, name), k.buf(name)) for _ in range(n)])


TWO_PI = 2.0 * math.pi


def _split_2pi():
    def trunc_bits(v, bits):
        m, e = math.frexp(v)
        m = math.floor(m * (1 << bits)) / (1 << bits)
        return math.ldexp(m, e)
    c1 = trunc_bits(TWO_PI, 8)
    c2 = trunc_bits(TWO_PI - c1, 11)
    c3 = float(np.float32(TWO_PI - c1 - c2))
    return float(c1), float(c2), c3


C1, C2, C3 = _split_2pi()

P1_BLOCKS = [
    ("cB", 128), ("cC", 128), ("ch", 128), ("cg", 128),
    ("cq0", 128), ("cq1", 128), ("cq2", 128), ("cq3", 128),
    ("ckv0", 128), ("ckv1", 128), ("kpe", 64), ("kpr", 64),
    ("mg0", 128), ("mg1", 128),
    ("rr", 128), ("rk", 128), ("rv", 128), ("rwa", 128), ("rg", 128),
]
P1_OFF = {}
_o = 0
for _n, _w in P1_BLOCKS:
    P1_OFF[_n] = (_o, _w)
    _o += _w
P1_NC = _o

PP = {n: i for i, n in enumerate([
    "cw0", "cw1", "cw2", "qg0", "qg1", "qg2", "qg3", "kvg0", "kvg1",
    "mu_r", "mu_k", "mu_v", "mu_wa", "w0", "a0", "k_k", "k_a", "r_k", "gn_g", "gn_b", "invf",
    "om_r", "om_k", "om_v", "om_wa", "omk_a", "sgn",
])}
NPP = len(PP)
NPP_IN = PP["om_r"]


def build_p1(k, S, d, TT=512, stage=None):
    nc = k.nc
    NT = S // TT
    KC = D_MODEL // 128
    xT, w1, pp_d, wuq_d, wukv_d, w2a2_d, pos_d = d["xT"], d["w1"], d["pp"], d["wuq"], d["wukv"], d["w2a2"], d["pos"]

    pp = k.sb([128, NPP], F32, "pp")
    b_pp = k.buf("pp")
    ds_c = k.dsem("c")
    k.dma("sp", ds_c, lambda e: e.dma_start(out=pp[:, 0:NPP_IN], in_=pp_d[:, :]), writes=[b_pp])

    def ppc(name, lo=0, hi=128):
        return pp[lo:hi, PP[name]:PP[name] + 1]

    for src, dst in (("mu_r", "om_r"), ("mu_k", "om_k"), ("mu_v", "om_v"), ("mu_wa", "om_wa"), ("k_a", "omk_a")):
        k.op("dve", lambda e, s=src, t=dst: e.tensor_scalar(out=ppc(t), in0=ppc(s), scalar1=-1.0, scalar2=1.0,
                                                             op0=ALU.mult, op1=ALU.add), reads=[b_pp], writes=[b_pp])
    k.op("dve", lambda e: e.memset(pp[0:32, PP["sgn"]:PP["sgn"] + 1], -1.0), writes=[b_pp])
    k.op("dve", lambda e: e.memset(pp[32:64, PP["sgn"]:PP["sgn"] + 1], 1.0), writes=[b_pp])

    ones_bf = k.sb([128, 128], BF16, "ones")
    b_ones = k.buf("ones")
    k.op("dve", lambda e: e.memset(ones_bf[:], 1.0), writes=[b_ones])
    eps_t = k.sb([128, 1], F32, "eps")
    b_eps = k.buf("eps")
    k.op("dve", lambda e: e.memset(eps_t[:], 1e-6), writes=[b_eps])
    bones_bf = k.sb([128, 128], BF16, "bones")
    b_bones = k.buf("bones")
    k.op("dve", lambda e: e.memset(bones_bf[:], 0.0), writes=[b_bones])
    k.op("dve", lambda e: e.memset(bones_bf[0:64, 0:64], 1.0), writes=[b_bones])
    k.op("dve", lambda e: e.memset(bones_bf[64:128, 64:128], 1.0), writes=[b_bones])

    wbf = k.sb([128, KC, P1_NC], BF16, "wbf")
    b_wbf = k.buf("wbf")
    HW = P1_NC // 2
    stg = [(k.sb([128, HW], F32, "wstg"), k.buf("wstg"), k.dsem("wst")) for _ in range(2)]
    for c in range(KC):
        for hh in range(2):
            st, bst, dss = stg[hh]
            k.dma("sp", dss, lambda e, c=c, st=st, hh=hh: e.dma_start(out=st[:], in_=w1[c * 128:(c + 1) * 128, hh * HW:(hh + 1) * HW]), writes=[bst])
            eng = "dve" if hh == 0 else "pool"
            k.op(eng, lambda e, c=c, st=st, hh=hh: e.tensor_copy(out=wbf[:, c, hh * HW:(hh + 1) * HW], in_=st[:]), reads=[bst], writes=[b_wbf])

    wuq = k.sb([128, 4, 512], BF16, "wuq")
    b_wuq = k.buf("wuq")
    qscale = 192.0 ** -0.5
    for c in range(4):
        st, bst, dss = stg[c % 2]
        k.dma("sp", dss, lambda e, c=c, st=st: e.dma_start(out=st[:, 0:512], in_=wuq_d[c * 128:(c + 1) * 128, :]), writes=[bst])
        k.op("dve", lambda e, c=c, st=st: e.tensor_scalar(out=wuq[:, c, :], in0=st[:, 0:512], scalar1=ppc("qg%d" % c),
                                                          scalar2=qscale, op0=ALU.mult, op1=ALU.mult),
             reads=[bst, b_pp], writes=[b_wuq])
    wukv = k.sb([128, 2, 512], BF16, "wukv")
    b_wukv = k.buf("wukv")
    for c in range(2):
        st, bst, dss = stg[c % 2]
        k.dma("sp", dss, lambda e, c=c, st=st: e.dma_start(out=st[:, 0:512], in_=wukv_d[c * 128:(c + 1) * 128, :]), writes=[bst])
        k.op("dve", lambda e, c=c, st=st: e.tensor_scalar(out=wukv[:, c, :], in0=st[:, 0:512], scalar1=ppc("kvg%d" % c),
                                                          scalar2=None, op0=ALU.mult), reads=[bst, b_pp], writes=[b_wukv])
    w2a2 = k.sb([128, 128], BF16, "w2a2")
    b_w2a2 = k.buf("w2a2")
    st, bst, dss = stg[0]
    k.dma("sp", dss, lambda e, st=st: e.dma_start(out=st[:, 0:128], in_=w2a2_d[:, :]), writes=[bst])
    k.op("dve", lambda e, st=st: e.tensor_copy(out=w2a2[:], in_=st[:, 0:128]), reads=[bst], writes=[b_w2a2])

    xt_rot = [(k.sb([128, KC, TT], BF16, "xt"), k.buf("xt"), k.dsem("xt")) for _ in range(2)]
    psum = Rot([(k.ps([128, TT], F32, "ps"), k.pbuf("ps")) for _ in range(8)])
    f32t = sb_rot(k, 12, [128, TT], F32, "f")
    cs_rot = sb_rot(k, 4, [64, TT], F32, "cs")
    bf16t = sb_rot(k, 8, [128, TT], BF16, "h")
    obf = [(k.sb([128, TT], BF16, "ob"), k.buf("ob"), k.dsem("ob")) for _ in range(6)]
    of32 = [(k.sb([128, TT], F32, "of"), k.buf("of"), k.dsem("of")) for _ in range(8)]
    obf_i = [0]
    of32_i = [0]

    def next_obf():
        it = obf[obf_i[0] % len(obf)]
        obf_i[0] += 1
        return it

    def next_of32():
        it = of32[of32_i[0] % len(of32)]
        of32_i[0] += 1
        return it

    U1 = (k.sb([128, TT + 2], F32, "U"), k.buf("U"))
    U = [U1, U1]
    RAW = {}
    for n in ("rr", "rk", "rv", "rwa"):
        r1 = (k.sb([128, TT + 1], F32, "raw"), k.buf("raw"))
        RAW[n] = [r1, r1]
    k.op("pool", lambda e: e.memset(U[1][0][:, TT:TT + 2], 0.0), writes=[U[1][1]])
    for n in RAW:
        k.op("pool", lambda e, n=n: e.memset(RAW[n][1][0][:, TT:TT + 1], 0.0), writes=[RAW[n][1][1]])

    posi = k.sb([64, TT], I32, "posi")
    b_posi = k.buf("posi")
    ds_pos = k.dsem("pos")

    def xsrc(i, half):
        gq = d.get("xT_quarter")
        if gq is None:
            return xT[half * 1024:(half + 1) * 1024, i * TT:(i + 1) * TT]
        qi, off = (i * TT) // gq, (i * TT) % gq
        return xT[qi * D_MODEL + half * 1024:qi * D_MODEL + (half + 1) * 1024, off:off + TT]

    qcache1 = {}

    def load_x(i):
        xt, bxt, dsx = xt_rot[i % 2]
        for half in range(2):
            if d.get("xT_gather8"):
                gq = d["xT_gather8"]
                qi, off = (i * TT) // gq, (i * TT) % gq

                def ldx(e, xt=xt, half=half, qi=qi, off=off):
                    if "view" not in qcache1:
                        pid = e.partition_id()
                        b4 = pid - (pid % 4)
                        qcache1["view"] = xT.rearrange("(rc p) t -> p rc t", p=128)[:, bass.ds(b4 * 16, 64), :]
                    view = qcache1["view"]
                    return e.dma_start(out=xt[:, half * 8:(half + 1) * 8, :],
                                       in_=view[:, qi * 16 + half * 8:qi * 16 + half * 8 + 8, off:off + TT])
                k.dma("sp", dsx, ldx, writes=[bxt])
                continue
            k.dma("sp", dsx, lambda e, xt=xt, i=i, half=half: e.dma_start(
                out=xt[:, half * 8:(half + 1) * 8, :],
                in_=xsrc(i, half).rearrange("(c p) t -> p c t", p=128)),
                writes=[bxt])

    def mm_block(i, name, xt, bxt):
        off, w = P1_OFF[name]
        pt, bp = psum.next()
        for c in range(KC):
            k.op("pe", lambda e, c=c, pt=pt, off=off, w=w, xt=xt: e.matmul(
                pt[0:w, :], lhsT=wbf[:, c, off:off + w], rhs=xt[:, c, :], start=(c == 0), stop=(c == KC - 1)),
                reads=[b_wbf, bxt], writes=[bp])
        return pt, bp

    def store(q, dkey, dst_ap, t, bt, is_output=True):
        k.dma(q, dkey, lambda e: e.dma_start(out=dst_ap, in_=t), reads=[bt], is_output=is_output)

    if stage == 'weights':
        return
    load_x(0)
    for i in range(NT):
        if i + 1 < NT:
            load_x(i + 1)
        xt, bxt, dsx = xt_rot[i % 2]
        tsl = slice(i * TT, (i + 1) * TT)

        k.dma("sp", ds_pos, lambda e, tsl=tsl: e.dma_start(out=posi[:], in_=pos_d[0:1, tsl].broadcast_to([64, TT])), writes=[b_posi])
        ang, b_ang = f32t.next()
        k.op("dve", lambda e, ang=ang: e.tensor_copy(out=ang[0:64, :], in_=posi[:]), reads=[b_posi], writes=[b_ang])
        k.op("dve", lambda e, ang=ang: e.tensor_scalar(out=ang[0:64, :], in0=ang[0:64, :], scalar1=ppc("invf", 0, 64), scalar2=None,
                                                       op0=ALU.mult), reads=[b_ang, b_pp], writes=[b_ang])
        kq, b_kq = f32t.next()
        ki, b_ki = f32t.next()
        k.op("dve", lambda e, ang=ang, kq=kq: e.tensor_scalar(out=kq[0:64, :], in0=ang[0:64, :], scalar1=1.0 / TWO_PI, scalar2=None,
                                                              op0=ALU.mult), reads=[b_ang], writes=[b_kq])
        k.op("dve", lambda e, kq=kq, ki=ki: e.tensor_copy(out=ki[0:64, :].bitcast(I32), in_=kq[0:64, :]), reads=[b_kq], writes=[b_ki])
        k.op("dve", lambda e, kq=kq, ki=ki: e.tensor_copy(out=kq[0:64, :], in_=ki[0:64, :].bitcast(I32)), reads=[b_ki], writes=[b_kq])
        red, b_red = f32t.next()
        k.op("dve", lambda e, red=red, ang=ang, kq=kq: e.scalar_tensor_tensor(out=red[0:64, :], in0=kq[0:64, :], scalar=-C1, in1=ang[0:64, :],
                                                                            op0=ALU.mult, op1=ALU.add), reads=[b_ang, b_kq], writes=[b_red])
        k.op("dve", lambda e, red=red, kq=kq: e.scalar_tensor_tensor(out=red[0:64, :], in0=kq[0:64, :], scalar=-C2, in1=red[0:64, :],
                                                                   op0=ALU.mult, op1=ALU.add), reads=[b_red, b_kq], writes=[b_red])
        k.op("dve", lambda e, red=red, kq=kq: e.scalar_tensor_tensor(out=red[0:64, :], in0=kq[0:64, :], scalar=-C3, in1=red[0:64, :],
                                                                   op0=ALU.mult, op1=ALU.add), reads=[b_red, b_kq], writes=[b_red])
        sarg, b_sarg = f32t.next()
        carg, b_carg = f32t.next()

        def wrap(dst, bdst, src, bsrc, shift):
            PI_LO = 3.1415925
            k.op("dve", lambda e: e.tensor_scalar(out=dst[0:64, :], in0=src[0:64, :], scalar1=shift, scalar2=None, op0=ALU.add),
                 reads=[bsrc], writes=[bdst])
            for sgn_, cmp_ in ((-1.0, ALU.is_gt), (1.0, ALU.is_lt)):
                m, bm = f32t.next()
                k.op("dve", lambda e, m=m, sgn_=sgn_, cmp_=cmp_: e.tensor_scalar(out=m[0:64, :], in0=dst[0:64, :], scalar1=-sgn_ * math.pi,
                                                                               scalar2=sgn_ * TWO_PI, op0=cmp_, op1=ALU.mult),
                     reads=[bdst], writes=[bm])
                k.op("dve", lambda e, m=m: e.tensor_tensor(out=dst[0:64, :], in0=dst[0:64, :], in1=m[0:64, :], op=ALU.add),
                     reads=[bdst, bm], writes=[bdst])
            k.op("dve", lambda e: e.tensor_scalar(out=dst[0:64, :], in0=dst[0:64, :], scalar1=PI_LO, scalar2=-PI_LO, op0=ALU.min, op1=ALU.max),
                 reads=[bdst], writes=[bdst])

        wrap(sarg, b_sarg, red, b_red, 0.0)
        wrap(carg, b_carg, red, b_red, math.pi / 2)
        cos2, b_cos2 = cs_rot.next()
        sin2, b_sin2 = cs_rot.next()
        k.op("act", lambda e, sarg=sarg, sin2=sin2: e.activation(out=sin2[0:64, :], in_=sarg[0:64, :], func=AF.Sin,
                                                                  scale=ppc("sgn", 0, 64)), reads=[b_sarg, b_pp], writes=[b_sin2])
        k.op("act", lambda e, carg=carg, cos2=cos2: e.activation(out=cos2[0:64, :], in_=carg[0:64, :], func=AF.Sin),
             reads=[b_carg], writes=[b_cos2])

        if stage == 'rope':
            continue
        pC, bpC = mm_block(i, "cC", xt, bxt)
        Csb, bCsb = f32t.next()
        k.op("act", lambda e, Csb=Csb, pC=pC: e.activation(out=Csb[:], in_=pC[:], func=AF.Copy), reads=[bpC], writes=[bCsb])
        ph, bph = mm_block(i, "ch", xt, bxt)
        Ut, bU = U[i % 2]
        Up, bUp = U[(i + 1) % 2]
        k.op("pool", lambda e, Ut=Ut, Up=Up: e.tensor_copy(out=Ut[:, 0:2], in_=Up[:, TT:TT + 2]), reads=[bUp], writes=[bU])
        k.op("dve", lambda e, Ut=Ut, Csb=Csb, ph=ph: e.tensor_tensor(out=Ut[:, 2:TT + 2], in0=Csb[:], in1=ph[:], op=ALU.mult),
             reads=[bCsb, bph], writes=[bU])
        pB, bpB = mm_block(i, "cB", xt, bxt)
        Bsb, bBsb = f32t.next()
        k.op("act", lambda e, Bsb=Bsb, pB=pB: e.activation(out=Bsb[:], in_=pB[:], func=AF.Copy), reads=[bpB], writes=[bBsb])
        pg, bpg = mm_block(i, "cg", xt, bxt)
        sg, bsg = f32t.next()
        k.op("act", lambda e, sg=sg, pg=pg: e.activation(out=sg[:], in_=pg[:], func=AF.Silu), reads=[bpg], writes=[bsg])
        y, by = f32t.next()
        k.op("dve", lambda e, y=y, Ut=Ut: e.tensor_scalar(out=y[:], in0=Ut[:, 0:TT], scalar1=ppc("cw0"), scalar2=None, op0=ALU.mult),
             reads=[bU, b_pp], writes=[by])
        k.op("dve", lambda e, y=y, Ut=Ut: e.scalar_tensor_tensor(out=y[:], in0=Ut[:, 1:TT + 1], scalar=ppc("cw1"), in1=y[:],
                                                                 op0=ALU.mult, op1=ALU.add), reads=[bU, b_pp, by], writes=[by])
        k.op("dve", lambda e, y=y, Ut=Ut: e.scalar_tensor_tensor(out=y[:], in0=Ut[:, 2:TT + 2], scalar=ppc("cw2"), in1=y[:],
                                                                 op0=ALU.mult, op1=ALU.add), reads=[bU, b_pp, by], writes=[by])
        k.op("pool", lambda e, y=y, Bsb=Bsb: e.tensor_tensor(out=y[:], in0=y[:], in1=Bsb[:], op=ALU.mult), reads=[by, bBsb], writes=[by])
        ob, bob, dso = next_obf()
        k.op("pool", lambda e, y=y, sg=sg, ob=ob: e.tensor_tensor(out=ob[:], in0=y[:], in1=sg[:], op=ALU.mult), reads=[by, bsg], writes=[bob])
        store("sp", dso, mix_dst(d, "ycT", 0, i * TT, TT), ob[:], bob)

        if stage == 'conv':
            continue
        def rms_normed(names, dim, tagn):
            raws = []
            pss, bpss = psum.next()
            n = len(names)
            for j, nm in enumerate(names):
                pt, bp = mm_block(i, nm, xt, bxt)
                sq, bsq = bf16t.next()
                k.op("act", lambda e, sq=sq, pt=pt: e.activation(out=sq[:], in_=pt[:], func=AF.Square), reads=[bp], writes=[bsq])
                raw, braw = f32t.next()
                k.op("dve", lambda e, raw=raw, pt=pt: e.tensor_copy(out=raw[:], in_=pt[:]), reads=[bp], writes=[braw])
                raws.append((raw, braw))
                if _DBG_LEVEL >= 1:
                    k.op("pe", lambda e, sq=sq, j=j, pss=pss: e.matmul(pss[:], lhsT=ones_bf[:], rhs=sq[:], start=(j == 0), stop=(j == n - 1)),
                         reads=[b_ones, bsq], writes=[bpss])
            if _DBG_LEVEL < 2:
                return []
            rstd, brstd = f32t.next()
            k.op("act", lambda e, rstd=rstd, pss=pss: e.activation(out=rstd[:], in_=pss[:], func=AF.Ln, scale=1.0 / dim, bias=eps_t[:, 0:1]),
                 reads=[bpss, b_eps], writes=[brstd])
            k.op("act", lambda e, rstd=rstd: e.activation(out=rstd[:], in_=rstd[:], func=AF.Exp, scale=-0.5), reads=[brstd], writes=[brstd])
            outs = []
            if _DBG_LEVEL < 3:
                return []
            for raw, braw in raws:
                nb, bnb = bf16t.next()
                k.op("pool", lambda e, nb=nb, raw=raw, rstd=rstd: e.tensor_tensor(out=nb[:], in0=raw[:], in1=rstd[:], op=ALU.mult),
                     reads=[braw, brstd], writes=[bnb])
                outs.append((nb, bnb))
            return outs

        cqn = rms_normed(["cq0", "cq1", "cq2", "cq3"], 512.0, "q")
        if stage == "m1":
            continue
        for h in range(2):
            base = h * 256
            pt, bp = psum.next()
            for c in range(4):
                k.op("pe", lambda e, c=c, pt=pt, base=base: e.matmul(pt[:], lhsT=wuq[:, c, base:base + 128], rhs=cqn[c][0][:],
                                                                    start=(c == 0), stop=(c == 3)), reads=[b_wuq, cqn[c][1]], writes=[bp])
            ob, bob, dso = next_obf()
            k.op("act", lambda e, ob=ob, pt=pt: e.activation(out=ob[:], in_=pt[:], func=AF.Copy), reads=[bp], writes=[bob])
            store("sp", dso, d["QnT"][h, :, tsl], ob[:], bob)
            pa, bpa = psum.next()
            for c in range(4):
                k.op("pe", lambda e, c=c, pa=pa, base=base: e.matmul(pa[0:64, :], lhsT=wuq[:, c, base + 128:base + 192], rhs=cqn[c][0][:],
                                                                    start=(c == 0), stop=(c == 3)), reads=[b_wuq, cqn[c][1]], writes=[bpa])
            pb, bpb = psum.next()
            for c in range(4):
                k.op("pe", lambda e, c=c, pb=pb, base=base: e.matmul(pb[0:64, :], lhsT=wuq[:, c, base + 192:base + 256], rhs=cqn[c][0][:],
                                                                    start=(c == 0), stop=(c == 3)), reads=[b_wuq, cqn[c][1]], writes=[bpb])
            t1, bt1 = f32t.next()
            k.op("dve", lambda e, t1=t1, pa=pa: e.tensor_tensor(out=t1[0:64, :], in0=pa[0:64, :], in1=cos2[0:64, :], op=ALU.mult),
                 reads=[bpa, b_cos2], writes=[bt1])
            t2, bt2 = f32t.next()
            k.op("dve", lambda e, t2=t2, pb=pb: e.tensor_tensor(out=t2[0:64, :], in0=pb[0:64, :], in1=sin2[0:64, :], op=ALU.mult),
                 reads=[bpb, b_sin2], writes=[bt2])
            ob, bob, dso = next_obf()
            k.op("pool", lambda e, ob=ob, t1=t1, t2=t2: e.tensor_tensor(out=ob[0:64, :], in0=t1[0:64, :], in1=t2[0:64, :], op=ALU.add),
                 reads=[bt1, bt2], writes=[bob])
            store("sp", dso, d["QpT"][h, :, tsl], ob[0:64, :], bob)

        if stage == "m2":
            continue
        ckvn = rms_normed(["ckv0", "ckv1"], 256.0, "kv")
        for h in range(2):
            pt, bp = psum.next()
            for c in range(2):
                k.op("pe", lambda e, c=c, pt=pt, h=h: e.matmul(pt[:], lhsT=wukv[:, c, h * 128:(h + 1) * 128], rhs=ckvn[c][0][:],
                                                              start=(c == 0), stop=(c == 1)), reads=[b_wukv, ckvn[c][1]], writes=[bp])
            ob, bob, dso = next_obf()
            k.op("act", lambda e, ob=ob, pt=pt: e.activation(out=ob[:], in_=pt[:], func=AF.Copy), reads=[bp], writes=[bob])
            store("sp", dso, d["KnT"][h, :, tsl], ob[:], bob)
        if stage == "m3":
            continue
        for half in range(2):
            pt, bp = psum.next()
            for sbk in range(2):
                s4 = half * 2 + sbk
                for c in range(2):
                    k.op("pe", lambda e, c=c, pt=pt, s4=s4, sbk=sbk: e.matmul(
                        pt[:, sbk * 256:(sbk + 1) * 256], lhsT=ckvn[c][0][:, s4 * 128:(s4 + 1) * 128], rhs=wukv[:, c, 256:512],
                        start=(c == 0), stop=(c == 1)), reads=[b_wukv, ckvn[c][1]], writes=[bp])
            ob, bob, dso = next_obf()
            k.op("act", lambda e, ob=ob, pt=pt: e.activation(out=ob[:], in_=pt[:], func=AF.Copy), reads=[bp], writes=[bob])
            for h in range(2):
                blk0 = i * (TT // 128) + half * 2
                k.dma("sp", dso, lambda e, ob=ob, h=h, blk0=blk0: e.dma_start(
                    out=d["V"][h, :, blk0:blk0 + 2, :],
                    in_=ob[:].rearrange("t (s h v) -> t s h v", s=2, h=2)[:, :, h, :]), reads=[bob], is_output=True)
        if stage == "m4":
            continue
        pa, bpa = mm_block(i, "kpe", xt, bxt)
        pb, bpb = mm_block(i, "kpr", xt, bxt)
        t1, bt1 = f32t.next()
        k.op("dve", lambda e, t1=t1, pa=pa: e.tensor_tensor(out=t1[0:64, :], in0=pa[0:64, :], in1=cos2[0:64, :], op=ALU.mult),
             reads=[bpa, b_cos2], writes=[bt1])
        t2, bt2 = f32t.next()
        k.op("dve", lambda e, t2=t2, pb=pb: e.tensor_tensor(out=t2[0:64, :], in0=pb[0:64, :], in1=sin2[0:64, :], op=ALU.mult),
             reads=[bpb, b_sin2], writes=[bt2])
        ob, bob, dso = next_obf()
        k.op("pool", lambda e, ob=ob, t1=t1, t2=t2: e.tensor_tensor(out=ob[0:64, :], in0=t1[0:64, :], in1=t2[0:64, :], op=ALU.add),
             reads=[bt1, bt2], writes=[bob])
        store("sp", dso, d["KpT"][:, tsl], ob[0:64, :], bob)
        for h in range(2):
            pt, bp = mm_block(i, "mg%d" % h, xt, bxt)
            of, bof, dsf = next_of32()
            k.op("act", lambda e, of=of, pt=pt: e.activation(out=of[:], in_=pt[:], func=AF.Silu), reads=[bp], writes=[bof])
            store("sp", dsf, d["SgM"][h, :, tsl], of[:], bof)

        if stage == 'mla':
            continue
        xs = {}
        for nm, mu in (("rr", "r"), ("rk", "k"), ("rv", "v"), ("rwa", "wa")):
            pt, bp = mm_block(i, nm, xt, bxt)
            raw, braw = RAW[nm][i % 2]
            rawp, brawp = RAW[nm][(i + 1) % 2]
            k.op("pool", lambda e, raw=raw, rawp=rawp: e.tensor_copy(out=raw[:, 0:1], in_=rawp[:, TT:TT + 1]), reads=[brawp], writes=[braw])
            k.op("act", lambda e, raw=raw, pt=pt: e.activation(out=raw[:, 1:TT + 1], in_=pt[:], func=AF.Copy), reads=[bp], writes=[braw])
            x1, bx1 = f32t.next()
            k.op("dve", lambda e, x1=x1, raw=raw, mu=mu: e.tensor_scalar(out=x1[:], in0=raw[:, 1:TT + 1], scalar1=ppc("om_" + mu), scalar2=None,
                                                                        op0=ALU.mult), reads=[braw, b_pp], writes=[bx1])
            k.op("dve", lambda e, x1=x1, raw=raw, mu=mu: e.scalar_tensor_tensor(out=x1[:], in0=raw[:, 0:TT], scalar=ppc("mu_" + mu), in1=x1[:],
                                                                               op0=ALU.mult, op1=ALU.add), reads=[braw, b_pp, bx1], writes=[bx1])
            xs[nm] = (x1, bx1)
        r_, br_ = xs["rr"]
        k_, bk_ = xs["rk"]
        v_, bv_ = xs["rv"]
        wa_, bwa_ = xs["rwa"]
        th, bth = bf16t.next()
        k.op("act", lambda e, th=th: e.activation(out=th[0:64, :], in_=wa_[0:64, :], func=AF.Tanh), reads=[bwa_], writes=[bth])
        k.op("dve", lambda e, th=th: e.tensor_copy(out=th[64:128, :], in_=wa_[64:128, :]), reads=[bwa_], writes=[bth])
        pz, bpz = psum.next()
        k.op("pe", lambda e, pz=pz, th=th: e.matmul(pz[:], lhsT=w2a2[0:64, :], rhs=th[0:64, :], start=True, stop=True),
             reads=[b_w2a2, bth], writes=[bpz])
        pa2, bpa2 = psum.next()
        k.op("pe", lambda e, pa2=pa2, th=th: e.matmul(pa2[:], lhsT=w2a2[64:128, :], rhs=th[64:128, :], start=True, stop=True),
             reads=[b_w2a2, bth], writes=[bpa2])
        sgm, bsgm, ds_sgm = next_of32()
        k.op("act", lambda e, sgm=sgm, pz=pz: e.activation(out=sgm[:], in_=pz[:], func=AF.Sigmoid, bias=ppc("w0")), reads=[bpz, b_pp], writes=[bsgm])
        store("sp", ds_sgm, d["RW"][5, :, tsl], sgm[:], bsgm)
        a_, ba_ = f32t.next()
        k.op("act", lambda e, a_=a_, pa2=pa2: e.activation(out=a_[:], in_=pa2[:], func=AF.Sigmoid, bias=ppc("a0")), reads=[bpa2, b_pp], writes=[ba_])
        kk0, bkk0 = f32t.next()
        k.op("dve", lambda e, kk0=kk0: e.tensor_scalar(out=kk0[:], in0=k_[:], scalar1=ppc("k_k"), scalar2=None, op0=ALU.mult),
             reads=[bk_, b_pp], writes=[bkk0])
        sqk, bsqk = bf16t.next()
        k.op("act", lambda e, sqk=sqk, kk0=kk0: e.activation(out=sqk[:], in_=kk0[:], func=AF.Square), reads=[bkk0], writes=[bsqk])
        pn, bpn = psum.next()
        k.op("pe", lambda e, pn=pn, sqk=sqk: e.matmul(pn[:], lhsT=bones_bf[:], rhs=sqk[:], start=True, stop=True), reads=[b_bones, bsqk], writes=[bpn])
        rn, brn = f32t.next()
        k.op("dve", lambda e, rn=rn, pn=pn: e.tensor_scalar(out=rn[:], in0=pn[:], scalar1=1e-24, scalar2=None, op0=ALU.max), reads=[bpn], writes=[brn])
        k.op("act", lambda e, rn=rn: e.activation(out=rn[:], in_=rn[:], func=AF.Ln), reads=[brn], writes=[brn])
        k.op("act", lambda e, rn=rn: e.activation(out=rn[:], in_=rn[:], func=AF.Exp, scale=-0.5), reads=[brn], writes=[brn])
        kk, bkk, ds_kk = next_of32()
        k.op("pool", lambda e, kk=kk, kk0=kk0, rn=rn: e.tensor_tensor(out=kk[:], in0=kk0[:], in1=rn[:], op=ALU.mult), reads=[bkk0, brn], writes=[bkk])
        store("sp", ds_kk, d["RW"][3, :, tsl], kk[:], bkk)
        bb, bbb, ds_bb = next_of32()
        k.op("pool", lambda e, bb=bb, kk=kk, a_=a_: e.tensor_tensor(out=bb[:], in0=kk[:], in1=a_[:], op=ALU.mult), reads=[bkk, ba_], writes=[bbb])
        store("sp", ds_bb, d["RW"][4, :, tsl], bb[:], bbb)
        f_, bf_ = f32t.next()
        k.op("dve", lambda e, f_=f_, a_=a_: e.tensor_scalar(out=f_[:], in0=a_[:], scalar1=ppc("k_a"), scalar2=ppc("omk_a"), op0=ALU.mult, op1=ALU.add),
             reads=[ba_, b_pp], writes=[bf_])
        kp, bkp, ds_kp = next_of32()
        k.op("dve", lambda e, kp=kp, f_=f_: e.tensor_tensor(out=kp[:], in0=k_[:], in1=f_[:], op=ALU.mult), reads=[bk_, bf_], writes=[bkp])
        store("sp", ds_kp, d["RW"][1, :, tsl], kp[:], bkp)
        ro, bro, ds_ro = next_of32()
        k.op("pool", lambda e, ro=ro: e.tensor_copy(out=ro[:], in_=r_[:]), reads=[br_], writes=[bro])
        store("sp", ds_ro, d["RW"][0, :, tsl], ro[:], bro)
        vo, bvo, ds_vo = next_of32()
        k.op("pool", lambda e, vo=vo: e.tensor_copy(out=vo[:], in_=v_[:]), reads=[bv_], writes=[bvo])
        store("sp", ds_vo, d["RW"][2, :, tsl], vo[:], bvo)
        rk, brk = bf16t.next()
        k.op("dve", lambda e, rk=rk, kp=kp: e.scalar_tensor_tensor(out=rk[:], in0=r_[:], scalar=ppc("r_k"), in1=kp[:], op0=ALU.mult, op1=ALU.mult),
             reads=[br_, bkp, b_pp], writes=[brk])
        pbn, bpbn = psum.next()
        k.op("pe", lambda e, pbn=pbn, rk=rk: e.matmul(pbn[:], lhsT=bones_bf[:], rhs=rk[:], start=True, stop=True), reads=[b_bones, brk], writes=[bpbn])
        bo, bbo, ds_bo = next_of32()
        k.op("dve", lambda e, bo=bo, pbn=pbn: e.tensor_tensor(out=bo[:], in0=pbn[:], in1=v_[:], op=ALU.mult), reads=[bpbn, bv_], writes=[bbo])
        store("sp", ds_bo, d["RW"][6, :, tsl], bo[:], bbo)
        pt, bp = mm_block(i, "rg", xt, bxt)
        so, bso, ds_so = next_of32()
        k.op("act", lambda e, so=so, pt=pt: e.activation(out=so[:], in_=pt[:], func=AF.Silu), reads=[bp], writes=[bso])
        store("sp", ds_so, d["RW"][7, :, tsl], so[:], bso)


def p1_dram(nc, S, kind_in="ExternalInput", kind_out="ExternalOutput"):
    d = {}
    d["xT"] = nc.dram_tensor("xT", [D_MODEL, S], BF16, kind=kind_in).ap()
    d["w1"] = nc.dram_tensor("w1", [D_MODEL, P1_NC], F32, kind="ExternalInput").ap()
    d["pp"] = nc.dram_tensor("pp", [128, NPP_IN], F32, kind="ExternalInput").ap()
    d["wuq"] = nc.dram_tensor("wuq", [512, 512], F32, kind="ExternalInput").ap()
    d["wukv"] = nc.dram_tensor("wukv", [256, 512], F32, kind="ExternalInput").ap()
    d["w2a2"] = nc.dram_tensor("w2a2", [128, 128], F32, kind="ExternalInput").ap()
    d["pos"] = nc.dram_tensor("pos", [1, S], I32, kind="ExternalInput").ap()
    d["ycT"] = nc.dram_tensor("ycT", [128, S], BF16, kind=kind_out).ap()
    d["QnT"] = nc.dram_tensor("QnT", [2, 128, S], BF16, kind=kind_out).ap()
    d["QpT"] = nc.dram_tensor("QpT", [2, 64, S], BF16, kind=kind_out).ap()
    d["KnT"] = nc.dram_tensor("KnT", [2, 128, S], BF16, kind=kind_out).ap()
    d["KpT"] = nc.dram_tensor("KpT", [64, S], BF16, kind=kind_out).ap()
    d["V"] = nc.dram_tensor("V", [2, 128, S // 128, 128], BF16, kind=kind_out).ap()
    d["SgM"] = nc.dram_tensor("SgM", [2, 128, S], F32, kind=kind_out).ap()
    d["RW"] = nc.dram_tensor("RW", [8, 128, S], F32, kind=kind_out).ap()
    return d


IN_SIZES = (512, 512, 512, 512, 512, 256, 64, 1024, 1664, 512)
IN_OFFS = np.concatenate([[0], np.cumsum(IN_SIZES)]).astype(int)
INV_FREQ = (10000.0 ** (-np.arange(0, 64, 2, dtype=np.float32) / 64)).astype(np.float32)


def host_layer_params(inp, l, g):
    w_in = inp["w_in"][l]
    o = IN_OFFS
    cols = []
    for j in range(4):
        cols.append(np.arange(o[j] + 128 * g, o[j] + 128 * g + 128))
    cols.append(np.arange(o[4], o[4] + 512))
    cols.append(np.arange(o[5], o[5] + 256))
    kpe = np.arange(o[6], o[6] + 64)
    cols.append(kpe)
    cols.append(np.concatenate([kpe[32:], kpe[:32]]))
    cols.append(np.arange(o[7] + 256 * g, o[7] + 256 * g + 256))
    rb = o[8]
    cols.append(np.arange(rb + 128 * g, rb + 128 * g + 128))
    cols.append(np.arange(rb + 576 + 128 * g, rb + 576 + 128 * g + 128))
    cols.append(np.arange(rb + 1088 + 128 * g, rb + 1088 + 128 * g + 128))
    cols.append(np.arange(rb + 512, rb + 576))
    cols.append(np.arange(rb + 1600, rb + 1664))
    cols.append(np.arange(o[9] + 128 * g, o[9] + 128 * g + 128))
    cols = np.concatenate(cols)
    assert cols.shape[0] == P1_NC
    w1 = np.ascontiguousarray(w_in[:, cols])
    pp = np.zeros((128, NPP_IN), np.float32)
    sl = slice(128 * g, 128 * g + 128)
    for j in range(3):
        pp[:, PP["cw%d" % j]] = inp["conv_w"][l, j, sl]
    for c in range(4):
        pp[:, PP["qg%d" % c]] = inp["q_norm_g"][l, c * 128:(c + 1) * 128]
    for c in range(2):
        pp[:, PP["kvg%d" % c]] = inp["kv_norm_g"][l, c * 128:(c + 1) * 128]
    mu = inp["rwkv_mu"][l]
    pp[:, PP["mu_r"]] = mu[0:512][sl]
    pp[:, PP["mu_k"]] = mu[576:1088][sl]
    pp[:, PP["mu_v"]] = mu[1088:1600][sl]
    pp[0:64, PP["mu_wa"]] = mu[512:576]
    pp[64:128, PP["mu_wa"]] = mu[1600:1664]
    pp[:, PP["w0"]] = inp["rwkv_w0"][l, sl]
    pp[:, PP["a0"]] = inp["rwkv_a0"][l, sl]
    pp[:, PP["k_k"]] = inp["rwkv_k_k"][l, sl]
    pp[:, PP["k_a"]] = inp["rwkv_k_a"][l, sl]
    pp[:, PP["r_k"]] = inp["rwkv_r_k"][l].reshape(-1)[sl]
    pp[:, PP["gn_g"]] = inp["rwkv_gn_g"][l, sl]
    pp[:, PP["gn_b"]] = inp["rwkv_gn_b"][l, sl]
    pp[0:32, PP["invf"]] = INV_FREQ
    pp[32:64, PP["invf"]] = INV_FREQ
    wq = inp["w_uq"][l]
    qc = []
    for h in (2 * g, 2 * g + 1):
        b0 = h * 192
        pe = np.arange(b0 + 128, b0 + 192)
        qc += [np.arange(b0, b0 + 128), pe, np.concatenate([pe[32:], pe[:32]])]
    wuq = np.ascontiguousarray(wq[:, np.concatenate(qc)])
    wkv = inp["w_ukv"][l]
    kc = []
    for h in (2 * g, 2 * g + 1):
        kc.append(np.arange(h * 256, h * 256 + 128))
    for h in (2 * g, 2 * g + 1):
        kc.append(np.arange(h * 256 + 128, h * 256 + 256))
    wukv = np.ascontiguousarray(wkv[:, np.concatenate(kc)])
    w2a2 = np.ascontiguousarray(np.concatenate([inp["rwkv_w2"][l][:, sl], inp["rwkv_a2"][l][:, sl]], axis=0))
    return {"w1": w1, "pp": pp, "wuq": wuq, "wukv": wukv, "w2a2": w2a2}


def build_p3(k, S, d, QC=512):
    nc = k.nc
    NQ = S // QC
    NB = S // 128
    ones_bf = k.sb([128, 128], BF16, "ones")
    b_ones = k.buf("ones")
    k.op("dve", lambda e: e.memset(ones_bf[:], 1.0), writes=[b_ones])
    masks = []
    for m in range(QC // 128):
        mi = k.sb([128, QC], I32, "mi")
        bmi = k.buf("mi")
        k.op("pool", lambda e, mi=mi, m=m: e.iota(mi[:], pattern=[[1, QC]], base=-128 * m, channel_multiplier=-1), writes=[bmi])
        mb = k.sb([128, QC], BF16, "mb")
        bmb = k.buf("mb")
        k.op("dve", lambda e, mi=mi, mb=mb: e.tensor_scalar(out=mb[:], in0=mi[:], scalar1=0.0, scalar2=None, op0=ALU.is_ge),
             reads=[bmi], writes=[bmb])
        masks.append((mb, bmb))
    KnT = k.sb([128, S], BF16, "KnT")
    bKn = k.buf("KnT")
    dsKn = k.dsem("kn")
    KpT = k.sb([64, S], BF16, "KpT")
    bKp = k.buf("KpT")
    dsKp = k.dsem("kp")
    Vt = k.sb([128, NB, 128], BF16, "V")
    bV = k.buf("V")
    dsV = k.dsem("v")
    NSPL = max(1, S // 2048)
    CS = S // NSPL
    for c in range(NSPL):
        k.dma("sp", dsKp, lambda e, c=c: e.dma_start(out=KpT[:, c * CS:(c + 1) * CS], in_=d["KpT"][:, c * CS:(c + 1) * CS]), writes=[bKp])
    qin = [(k.sb([128, QC], BF16, "qn"), k.sb([64, QC], BF16, "qp"), k.sb([128, QC], F32, "sg"), k.buf("qin"), k.dsem("qin")) for _ in range(2)]
    pS = Rot([(k.ps([128, QC], F32, "pS"), k.pbuf("pS")) for _ in range(3)])
    pO = Rot([(k.ps([128, QC], F32, "pO"), k.pbuf("pO")) for _ in range(2)])
    pL = Rot([(k.ps([128, QC], F32, "pL"), k.pbuf("pL")) for _ in range(2)])
    Pt = sb_rot(k, 4, [128, QC], BF16, "P")
    rc_rot = sb_rot(k, 2, [128, QC], F32, "rc")
    o_rot = sb_rot(k, 2, [128, QC], F32, "o")
    out_rot = [(k.sb([128, QC], BF16, "yo"), k.buf("yo"), k.dsem("yo")) for _ in range(2)]
    it = 0
    for h in range(2):
        for c in range(NSPL):
            k.dma("sp", dsKn, lambda e, c=c, h=h: e.dma_start(out=KnT[:, c * CS:(c + 1) * CS], in_=d["KnT"][h, :, c * CS:(c + 1) * CS]), writes=[bKn])
            nb = CS // 128
            k.dma("sp", dsV, lambda e, c=c, h=h, nb=nb: e.dma_start(out=Vt[:, c * nb:(c + 1) * nb, :], in_=d["V"][h, :, c * nb:(c + 1) * nb, :]), writes=[bV])

        def load_q(j, slot):
            qn, qp, sg, bq, dsq = qin[slot]
            sl = slice(j * QC, (j + 1) * QC)
            k.dma("sp", dsq, lambda e: e.dma_start(out=qn[:], in_=d["QnT"][h, :, sl]), writes=[bq])
            k.dma("sp", dsq, lambda e: e.dma_start(out=qp[:], in_=d["QpT"][h, :, sl]), writes=[bq])
            k.dma("sp", dsq, lambda e: e.dma_start(out=sg[:], in_=d["SgM"][h, :, sl]), writes=[bq])

        load_q(0, it % 2)
        for j in range(NQ):
            slot = it % 2
            it += 1
            if j + 1 < NQ:
                load_q(j + 1, it % 2)
            qn, qp, sg, bq, dsq = qin[slot]
            nkb = (j + 1) * (QC // 128)
            po, bpo = pO.next()
            pl, bpl = pL.next()
            sbanks = {}

            def emit_qk(kb):
                ps_, bps = pS.next()
                sbanks[kb] = (ps_, bps)
                ksl = slice(kb * 128, (kb + 1) * 128)
                k.op("pe", lambda e: e.matmul(ps_[:], lhsT=KnT[:, ksl], rhs=qn[:], start=True, stop=False), reads=[bKn, bq], writes=[bps])
                k.op("pe", lambda e: e.matmul(ps_[:], lhsT=KpT[:, ksl], rhs=qp[:], start=False, stop=True), reads=[bKp, bq], writes=[bps])

            emit_qk(0)
            for kb in range(nkb):
                if kb + 1 < nkb:
                    emit_qk(kb + 1)
                ps_, bps = sbanks.pop(kb)
                P, bP = Pt.next()
                k.op("act", lambda e, P=P, ps_=ps_: e.activation(out=P[:], in_=ps_[:], func=AF.Exp), reads=[bps], writes=[bP])
                m = kb - j * (QC // 128)
                if m >= 0:
                    mb, bmb = masks[m]
                    k.op("dve", lambda e, P=P, mb=mb: e.tensor_tensor(out=P[:], in0=P[:], in1=mb[:], op=ALU.mult), reads=[bP, bmb], writes=[bP])
                k.op("pe", lambda e, P=P, kb=kb: e.matmul(po[:], lhsT=Vt[:, kb, :], rhs=P[:], start=(kb == 0), stop=(kb == nkb - 1)),
                     reads=[bV, bP], writes=[bpo])
                k.op("pe", lambda e, P=P, kb=kb: e.matmul(pl[:], lhsT=ones_bf[:], rhs=P[:], start=(kb == 0), stop=(kb == nkb - 1)),
                     reads=[b_ones, bP], writes=[bpl])
            rc, brc = rc_rot.next()
            k.op("dve", lambda e, rc=rc: e.reciprocal(out=rc[:], in_=pl[:]), reads=[bpl], writes=[brc])
            o, bo = o_rot.next()
            k.op("dve", lambda e, o=o, rc=rc: e.tensor_tensor(out=o[:], in0=po[:], in1=rc[:], op=ALU.mult), reads=[bpo, brc], writes=[bo])
            yo, byo, dsy = out_rot[j % 2]
            k.op("pool", lambda e, o=o, yo=yo: e.tensor_tensor(out=yo[:], in0=o[:], in1=sg[:], op=ALU.mult), reads=[bo, bq], writes=[byo])
            k.dma("sp", dsy, lambda e, yo=yo, j=j, h=h: e.dma_start(out=mix_dst(d, "ymT", 128 + 128 * h, j * QC, QC, h=h), in_=yo[:]), reads=[byo], is_output=True)


def p3_dram(nc, S, kind_in="ExternalInput", kind_out="ExternalOutput"):
    d = {}
    d["QnT"] = nc.dram_tensor("QnT", [2, 128, S], BF16, kind=kind_in).ap()
    d["QpT"] = nc.dram_tensor("QpT", [2, 64, S], BF16, kind=kind_in).ap()
    d["KnT"] = nc.dram_tensor("KnT", [2, 128, S], BF16, kind=kind_in).ap()
    d["KpT"] = nc.dram_tensor("KpT", [64, S], BF16, kind=kind_in).ap()
    d["V"] = nc.dram_tensor("V", [2, 128, S // 128, 128], BF16, kind=kind_in).ap()
    d["SgM"] = nc.dram_tensor("SgM", [2, 128, S], F32, kind=kind_in).ap()
    d["ymT"] = nc.dram_tensor("ymT", [2, 128, S], BF16, kind=kind_out).ap()
    return d


def build_p4(k, S, d, TT=512, L=128):
    nc = k.nc
    NT = S // TT
    NCH = TT // L
    pp = k.sb([128, NPP_IN], F32, "pp4")
    b_pp = k.buf("pp4")
    ds_c = k.dsem("c4")
    k.dma("sp", ds_c, lambda e: e.dma_start(out=pp[:], in_=d["pp"][:, :]), writes=[b_pp])

    def ppc(name):
        return pp[:, PP[name]:PP[name] + 1]

    mi = k.sb([128, 128], I32, "mi4")
    bmi = k.buf("mi4")
    k.op("pool", lambda e: e.iota(mi[:], pattern=[[1, 128]], base=0, channel_multiplier=-1), writes=[bmi])
    m_low = k.sb([128, 128], F32, "mlow"); m_up = k.sb([128, 128], F32, "mup"); m_upi = k.sb([128, 128], F32, "mupi")
    ident_bf = k.sb([128, 128], BF16, "identb"); ident_f = k.sb([128, 128], F32, "identf")
    b_m = k.buf("masks")
    k.op("dve", lambda e: e.tensor_scalar(out=m_low[:], in0=mi[:], scalar1=0.0, scalar2=None, op0=ALU.is_lt), reads=[bmi], writes=[b_m])
    k.op("dve", lambda e: e.tensor_scalar(out=m_up[:], in0=mi[:], scalar1=0.0, scalar2=None, op0=ALU.is_gt), reads=[bmi], writes=[b_m])
    k.op("dve", lambda e: e.tensor_scalar(out=m_upi[:], in0=mi[:], scalar1=0.0, scalar2=None, op0=ALU.is_ge), reads=[bmi], writes=[b_m])
    k.op("dve", lambda e: e.tensor_scalar(out=ident_bf[:], in0=mi[:], scalar1=0.0, scalar2=None, op0=ALU.is_equal), reads=[bmi], writes=[b_m])
    k.op("dve", lambda e: e.tensor_scalar(out=ident_f[:], in0=mi[:], scalar1=0.0, scalar2=None, op0=ALU.is_equal), reads=[bmi], writes=[b_m])
    rmask = k.sb([128, TT], F32, "rmask")
    b_rm = k.buf("rmask")
    k.op("dve", lambda e: e.memset(rmask[:], 1.0), writes=[b_rm])
    for c in range(NCH):
        k.op("dve", lambda e, c=c: e.memset(rmask[:, c * L:c * L + 1], 0.0), writes=[b_rm])
    eps_t = k.sb([128, 1], F32, "eps4")
    b_eps = k.buf("eps4")
    k.op("dve", lambda e: e.memset(eps_t[:], 64e-5), writes=[b_eps])
    H = k.sb([128, 64], F32, "H"); Hbf = k.sb([128, 64], BF16, "Hbf"); HE = k.sb([128, 64], F32, "HE")
    bH = [k.buf("H0"), k.buf("H1")]
    bHE = [k.buf("HE0"), k.buf("HE1")]
    k.op("dve", lambda e: e.memset(H[:], 0.0), writes=bH)
    k.op("dve", lambda e: e.memset(Hbf[:], 0.0), writes=bH)

    NAMES = ["r", "kp", "v", "kk", "b", "sgm", "bonusv", "sg_r"]
    inrot = [({n: k.sb([128, TT], F32, "in_" + n) for n in NAMES}, k.buf("in"), k.dsem("in4")) for _ in range(2)]
    f32t = sb_rot(k, 6, [128, TT], F32, "g")
    bft = sb_rot(k, 10, [128, TT], BF16, "gb")
    tokt = sb_rot(k, 8, [128, 128], BF16, "tok")
    mat = sb_rot(k, 24, [128, 128], BF16, "mat")
    matL = sb_rot(k, 12, [128, 128], BF16, "matL")
    smallb = sb_rot(k, 6, [128, 64], BF16, "sm")
    smallf = sb_rot(k, 6, [128, 64], F32, "smf")
    stat = sb_rot(k, 8, [128, 8], F32, "stat")
    psA = Rot([(k.ps([128, 512], F32, "psA")[:, 0:128], k.pbuf("psA")) for _ in range(3)])
    psB = Rot([(k.ps([128, 512], F32, "psB")[:, 0:128], k.pbuf("psB")) for _ in range(1)])
    psHs = [(k.ps([128, 512], F32, "psH"), k.pbuf("psH")) for _ in range(2)]
    psT = Rot([(k.ps([128, 1024], BF16, "psT")[:, 0:128], k.pbuf("psT")) for _ in range(2)])
    outt = [(k.sb([128, TT], BF16, "yo4"), k.buf("yo4"), k.dsem("yo4")) for _ in range(2)]
    ytile = sb_rot(k, 2, [128, TT], F32, "yt4")
    ynrot = sb_rot(k, 2, [128, 128], F32, "ynb")

    def load(i):
        tl, bt, ds = inrot[i % 2]
        for j, n in enumerate(NAMES):
            k.dma("sp", ds, lambda e, t=tl[n], j=j, i=i: e.dma_start(out=t[:], in_=d["RW"][j, :, i * TT:(i + 1) * TT]), writes=[bt])

    engs = ["dve", "pool"]
    load(0)
    for i in range(NT):
        if i + 1 < NT:
            load(i + 1)
        tl, bt, ds = inrot[i % 2]
        lw, blw = f32t.next()
        k.op("dve", lambda e: e.tensor_scalar(out=lw[:], in0=tl["sgm"][:], scalar1=-math.exp(-0.5), scalar2=None, op0=ALU.mult), reads=[bt], writes=[blw])
        cs, bcs = f32t.next()
        k.op("dve", lambda e: e.tensor_tensor_scan(out=cs[:], data0=rmask[:], data1=lw[:], initial=0.0, op0=ALU.mult, op1=ALU.add),
             reads=[b_rm, blw], writes=[bcs])
        Ep, bEp = f32t.next()
        k.op("act", lambda e: e.activation(out=Ep[:], in_=cs[:], func=AF.Exp), reads=[bcs], writes=[bEp])
        En, bEn = f32t.next()
        k.op("act", lambda e: e.activation(out=En[:], in_=cs[:], func=AF.Exp, scale=-1.0), reads=[bcs], writes=[bEn])
        k.op("pool", lambda e: e.tensor_tensor(out=lw[:], in0=cs[:], in1=lw[:], op=ALU.subtract), reads=[bcs, blw], writes=[blw])
        Epv, bEpv = f32t.next()
        k.op("act", lambda e: e.activation(out=Epv[:], in_=lw[:], func=AF.Exp), reads=[blw], writes=[bEpv])
        rt, brt = bft.next()
        k.op("dve", lambda e: e.tensor_tensor(out=rt[:], in0=tl["r"][:], in1=Ep[:], op=ALU.mult), reads=[bt, bEp], writes=[brt])
        kt, bkt = bft.next()
        k.op("pool", lambda e: e.tensor_tensor(out=kt[:], in0=tl["kp"][:], in1=En[:], op=ALU.mult), reads=[bt, bEn], writes=[bkt])
        btl, bbt = bft.next()
        k.op("dve", lambda e: e.tensor_tensor(out=btl[:], in0=tl["b"][:], in1=En[:], op=ALU.mult), reads=[bt, bEn], writes=[bbt])
        at, bat = bft.next()
        k.op("dve", lambda e: e.scalar_tensor_tensor(out=at[:], in0=tl["kk"][:], scalar=-1.0, in1=Epv[:], op0=ALU.mult, op1=ALU.mult),
             reads=[bt, bEpv], writes=[bat])
        vb, bvb = bft.next()
        k.op("pool", lambda e: e.tensor_copy(out=vb[:], in_=tl["v"][:]), reads=[bt], writes=[bvb])
        yt, byt = ytile.next()
        for c in range(NCH):
            csl = slice(c * L, (c + 1) * L)
            toks = []
            for src, bsrc in ((kt, bkt), (btl, bbt), (vb, bvb)):
                pt, bpt = psT.next()
                k.op("pe", lambda e: e.transpose(pt[:], src[:, csl], ident_bf[:]), reads=[bsrc, b_m], writes=[bpt])
                tk, btk = tokt.next()
                k.op("act", lambda e: e.activation(out=tk[:], in_=pt[:], func=AF.Copy), reads=[bpt], writes=[btk])
                toks.append((tk, btk))
            (ktT, bktT), (btT, bbtT), (vT, bvT) = toks
            ynb, bynb = ynrot.next()
            def head_gen(h):
                cp = slice(64 * h, 64 * h + 64)
                psH, bpsH = psHs[h]
                ei = [0]

                def eng():
                    ei[0] += 1
                    return engs[ei[0] % 2]

                def amat(l_, bl, r_, br, mask, pool=None):
                    pa, bpa = psA.next()
                    k.op("pe", lambda e: e.matmul(pa[:], lhsT=l_[cp, csl], rhs=r_[cp, csl], start=True, stop=True), reads=[bl, br], writes=[bpa])
                    m, bm = (pool or mat).next()
                    k.op("dve", lambda e: e.tensor_tensor(out=m[:], in0=pa[:], in1=mask[:], op=ALU.mult), reads=[bpa, b_m], writes=[bm])
                    return m, bm
                P, bP = amat(at, bat, btl, bbt, m_low)
                PT, bPT = amat(btl, bbt, at, bat, m_up)
                AakT, bAak = amat(kt, bkt, at, bat, m_up, matL)
                ArbT, bArb = amat(btl, bbt, rt, brt, m_upi, matL)
                ArkT, bArk = amat(kt, bkt, rt, brt, m_upi, matL)
                TTm, bTT = mat.next()
                k.op(eng(), lambda e: e.tensor_tensor(out=TTm[:], in0=PT[:], in1=ident_f[:], op=ALU.add), reads=[bPT, b_m], writes=[bTT])
                nit = 6
                for it_ in range(nit):
                    pa, bpa = psA.next()
                    k.op("pe", lambda e: e.matmul(pa[:], lhsT=PT[:], rhs=P[:], start=True, stop=True), reads=[bPT, bP], writes=[bpa])
                    P2, bP2 = mat.next()
                    k.op("act", lambda e: e.activation(out=P2[:], in_=pa[:], func=AF.Copy), reads=[bpa], writes=[bP2])
                    if it_ < nit - 1:
                        pb_, bpb_ = psA.next()
                        k.op("pe", lambda e: e.matmul(pb_[:], lhsT=P[:], rhs=PT[:], start=True, stop=True), reads=[bPT, bP], writes=[bpb_])
                        PT2, bPT2 = mat.next()
                        k.op("act", lambda e: e.activation(out=PT2[:], in_=pb_[:], func=AF.Copy), reads=[bpb_], writes=[bPT2])
                    yield
                    pc_, bpc_ = psA.next()
                    k.op("pe", lambda e: e.matmul(pc_[:], lhsT=P2[:], rhs=TTm[:], start=True, stop=True), reads=[bP2, bTT], writes=[bpc_])
                    TT2, bTT2 = mat.next()
                    k.op("dve", lambda e: e.tensor_tensor(out=TT2[:], in0=pc_[:], in1=TTm[:], op=ALU.add), reads=[bpc_, bTT], writes=[bTT2])
                    P, bP = P2, bP2
                    if it_ < nit - 1:
                        PT, bPT = PT2, bPT2
                    TTm, bTT = TT2, bTT2
                    yield
                vh = vT[:, 64 * h:64 * h + 64]
                E_L = Ep[:, c * L + L - 1:c * L + L]
                k.op("pool", lambda e: e.tensor_scalar(out=HE[cp, :], in0=H[cp, :], scalar1=E_L[cp, :], scalar2=None, op0=ALU.mult),
                     reads=[bH[h], bEp], writes=[bHE[h]])
                px, bpx = psH[:, 0:64], bpsH
                k.op("pe", lambda e: e.matmul(px, lhsT=AakT[:], rhs=vh, start=True, stop=False), reads=[bAak, bvT], writes=[bpx])
                k.op("pe", lambda e: e.matmul(px, lhsT=at[cp, csl], rhs=Hbf[cp, :], start=False, stop=True), reads=[bat, bH[h]], writes=[bpx])
                Xb, bXb = smallb.next()
                k.op("act", lambda e: e.activation(out=Xb[:], in_=px, func=AF.Copy), reads=[bpx], writes=[bXb])
                yield
                pu, bpu = psH[:, 64:128], bpsH
                k.op("pe", lambda e: e.matmul(pu, lhsT=TTm[:], rhs=Xb[:], start=True, stop=True), reads=[bTT, bXb], writes=[bpu])
                Ub, bUb = smallb.next()
                k.op("dve", lambda e: e.tensor_copy(out=Ub[:], in_=pu), reads=[bpu], writes=[bUb])
                yield
                py, bpy = psH[:, 128:192], bpsH
                k.op("pe", lambda e: e.matmul(py, lhsT=ArkT[:], rhs=vh, start=True, stop=False), reads=[bArk, bvT], writes=[bpy])
                k.op("pe", lambda e: e.matmul(py, lhsT=rt[cp, csl], rhs=Hbf[cp, :], start=False, stop=False), reads=[brt, bH[h]], writes=[bpy])
                k.op("pe", lambda e: e.matmul(py, lhsT=ArbT[:], rhs=Ub[:], start=False, stop=True), reads=[bArb, bUb], writes=[bpy])
                ph, bph = psH[:, 192:256], bpsH
                k.op("pe", lambda e: e.matmul(ph, lhsT=ktT[:], rhs=vh, start=True, stop=False), reads=[bktT, bvT], writes=[bph])
                k.op("pe", lambda e: e.matmul(ph, lhsT=btT[:], rhs=Ub[:], start=False, stop=True), reads=[bbtT, bUb], writes=[bph])
                k.op("dve", lambda e: e.scalar_tensor_tensor(out=H[cp, :], in0=psH[cp, 192:256], scalar=E_L[cp, :], in1=HE[cp, :], op0=ALU.mult, op1=ALU.add),
                     reads=[bph, bEp, bHE[h]], writes=[bH[h]])
                k.op("act", lambda e: e.activation(out=Hbf[cp, :], in_=H[cp, :], func=AF.Copy), reads=[bH[h]], writes=[bH[h]])
                yield
                st, bst = stat.next()
                k.op("dve", lambda e: e.bn_stats(out=st[:, 0:6], in_=py), reads=[bpy], writes=[bst])
                mv, bmv = stat.next()
                k.op("dve", lambda e: e.bn_aggr(out=mv[:, 0:2], in_=st[:, 0:6]), reads=[bst], writes=[bmv])
                k.op("act", lambda e: e.activation(out=mv[:, 2:3], in_=mv[:, 1:2], func=AF.Ln, bias=eps_t[:, 0:1]), reads=[bmv, b_eps], writes=[bmv])
                k.op("act", lambda e: e.activation(out=mv[:, 2:3], in_=mv[:, 2:3], func=AF.Exp, scale=-0.5), reads=[bmv], writes=[bmv])
                k.op("dve", lambda e: e.tensor_scalar(out=ynb[:, 64 * h:64 * h + 64], in0=py, scalar1=mv[:, 0:1], scalar2=mv[:, 2:3],
                                                      op0=ALU.subtract, op1=ALU.mult), reads=[bpy, bmv], writes=[bynb])
            gens = [head_gen(0), head_gen(1)]
            while gens:
                for g_ in list(gens):
                    try:
                        next(g_)
                    except StopIteration:
                        gens.remove(g_)
            pt2, bpt2 = psB.next()
            k.op("pe", lambda e: e.transpose(pt2[:], ynb[:], ident_f[:]), reads=[bynb, b_m], writes=[bpt2])
            k.op("act", lambda e: e.activation(out=yt[:, csl], in_=pt2[:], func=AF.Identity, scale=ppc("gn_g"), bias=ppc("gn_b")),
                 reads=[bpt2, b_pp], writes=[byt])
        k.op("dve", lambda e: e.tensor_tensor(out=yt[:], in0=yt[:], in1=tl["bonusv"][:], op=ALU.add), reads=[byt, bt], writes=[byt])
        yo, byo, dsy = outt[i % 2]
        k.op("pool", lambda e: e.tensor_tensor(out=yo[:], in0=yt[:], in1=tl["sg_r"][:], op=ALU.mult), reads=[byt, bt], writes=[byo])
        k.dma("sp", dsy, lambda e, i=i: e.dma_start(out=mix_dst(d, "yrT", 384, i * TT, TT), in_=yo[:]), reads=[byo], is_output=True)


def p4_dram(nc, S, kind_in="ExternalInput", kind_out="ExternalOutput"):
    d = {}
    d["RW"] = nc.dram_tensor("RW", [8, 128, S], F32, kind=kind_in).ap()
    d["pp"] = nc.dram_tensor("pp", [128, NPP_IN], F32, kind="ExternalInput").ap()
    d["yrT"] = nc.dram_tensor("yrT", [128, S], BF16, kind=kind_out).ap()
    return d


ALPHA = 4.0 ** 0.25


def build_p5(k, NTOK, d, only_transpose=False, chunk_map=None, do_transpose=True):
    nc = k.nc
    NBLK = NTOK // 128
    KC = D_MODEL // 128
    ident_f = k.sb([128, 128], F32, "identf5")
    mi = k.sb([128, 128], I32, "mi5")
    bmi = k.buf("mi5")
    b_id = k.buf("id5")
    k.op("pool", lambda e: e.iota(mi[:], pattern=[[1, 128]], base=0, channel_multiplier=-1), writes=[bmi])
    k.op("dve", lambda e: e.tensor_scalar(out=ident_f[:], in0=mi[:], scalar1=0.0, scalar2=None, op0=ALU.is_equal), reads=[bmi], writes=[b_id])
    if not only_transpose:
        eps_t = k.sb([128, 1], F32, "eps5")
        b_eps = k.buf("eps5")
        k.op("dve", lambda e: e.memset(eps_t[:], 1e-5), writes=[b_eps])
        wbf = k.sb([128, KC, D_MODEL], BF16, "wo")
        b_w = k.buf("wo")
        stg = [(k.sb([128, D_MODEL], F32, "wostg"), k.buf("wostg"), k.dsem("wost")) for _ in range(2)]
        for c in range(KC):
            st, bst, dss = stg[c % 2]
            k.dma("sp", dss, lambda e: e.dma_start(out=st[:], in_=d["w_out"][c * 128:(c + 1) * 128, :]), writes=[bst])
            k.op("dve" if c % 2 == 0 else "pool", lambda e: e.tensor_copy(out=wbf[:, c, :], in_=st[:]), reads=[bst], writes=[b_w])
        gbc = k.sb([128, D_MODEL], F32, "gbc")
        bbc = k.sb([128, D_MODEL], F32, "bbc")
        b_gb = k.buf("gb")
        ds_gb = k.dsem("gb")
        k.dma("sp", ds_gb, lambda e: e.dma_start(out=gbc[:], in_=d["ln_g"][0:1, :].broadcast_to([128, D_MODEL])), writes=[b_gb])
        k.dma("sp", ds_gb, lambda e: e.dma_start(out=bbc[:], in_=d["ln_b"][0:1, :].broadcast_to([128, D_MODEL])), writes=[b_gb])
        mixr = [(k.sb([128, KC, 128], BF16, "mx"), k.buf("mx"), k.dsem("mx")) for _ in range(2)]
        psO = [(k.ps([128, 512], F32, "psO"), k.pbuf("psO")) for _ in range(4)]
        statr = sb_rot(k, 4, [128, 32], F32, "st5")
    xr = [(k.sb([128, D_MODEL], F32, "x5"), k.buf("x5"), k.dsem("x5")) for _ in range(2)]
    yr = [(k.sb([128, D_MODEL], F32, "y5"), k.buf("y5"), k.dsem("y5")) for _ in range(2)]
    xTr = [(k.sb([128, KC, 128], BF16, "xT5"), k.buf("xT5"), k.dsem("xT5")) for _ in range(2)]
    psT = Rot([(k.ps([128, 512], F32, "psT5"), k.pbuf("psT5")) for _ in range(2)])

    qcache = {}

    def load(t):
        xt, bx, dsx = xr[t % 2]
        k.dma("sp", dsx, lambda e: e.dma_start(out=xt[:], in_=d["x"][t * 128:(t + 1) * 128, :]), writes=[bx])
        if not only_transpose:
            mx, bm, dsm = mixr[t % 2]
            if d.get("mix_dynamic"):
                def ldmix(e):
                    if "view" not in qcache:
                        pid = e.partition_id()
                        q = pid % 4
                        b4 = pid - q
                        qcache["view"] = d["mixT"].rearrange("(rc p) s -> p rc s", p=128)[:, bass.ds(b4 * 4, 16), bass.ds(q * NTOK, NTOK)]
                    view = qcache["view"]
                    return e.dma_start(out=mx[:], in_=view[:, :, t * 128:(t + 1) * 128])
                k.dma("sp", dsm, ldmix, writes=[bm])
            else:
                k.dma("sp", dsm, lambda e: e.dma_start(out=mx[:], in_=d["mixT"][:, t * 128:(t + 1) * 128].rearrange("(c p) t -> p c t", p=128)), writes=[bm])

    load(0)
    for t in range(NBLK):
        if t + 1 < NBLK:
            load(t + 1)
        xt, bx, dsx = xr[t % 2]
        if only_transpose:
            y, by = xt, bx
        else:
            mx, bm, dsm = mixr[t % 2]
            y, by, dsy = yr[t % 2]
            st, bst = statr.next()
            for nb in range(4):
                po, bpo = psO[nb]
                for c in range(KC):
                    k.op("pe", lambda e: e.matmul(po[:], lhsT=mx[:, c, :], rhs=wbf[:, (chunk_map[c] if chunk_map else c), nb * 512:(nb + 1) * 512], start=(c == 0), stop=(c == KC - 1)),
                         reads=[bm, b_w], writes=[bpo])
                k.op("dve", lambda e: e.scalar_tensor_tensor(out=y[:, nb * 512:(nb + 1) * 512], in0=xt[:, nb * 512:(nb + 1) * 512], scalar=ALPHA, in1=po[:],
                                                             op0=ALU.mult, op1=ALU.add), reads=[bx, bpo], writes=[by])
                k.op("dve", lambda e: e.bn_stats(out=st[:, nb * 6:(nb + 1) * 6], in_=y[:, nb * 512:(nb + 1) * 512]), reads=[by], writes=[bst])
            k.op("dve", lambda e: e.bn_aggr(out=st[:, 24:26], in_=st[:, 0:24]), reads=[bst], writes=[bst])
            k.op("act", lambda e: e.activation(out=st[:, 26:27], in_=st[:, 25:26], func=AF.Ln, bias=eps_t[:, 0:1]), reads=[bst, b_eps], writes=[bst])
            k.op("act", lambda e: e.activation(out=st[:, 26:27], in_=st[:, 26:27], func=AF.Exp, scale=-0.5), reads=[bst], writes=[bst])
            k.op("dve", lambda e: e.tensor_scalar(out=y[:], in0=y[:], scalar1=st[:, 24:25], scalar2=st[:, 26:27], op0=ALU.subtract, op1=ALU.mult),
                 reads=[by, bst], writes=[by])
            k.op("pool", lambda e: e.tensor_tensor(out=y[:], in0=y[:], in1=gbc[:], op=ALU.mult), reads=[by, b_gb], writes=[by])
            k.op("dve", lambda e: e.tensor_tensor(out=y[:], in0=y[:], in1=bbc[:], op=ALU.add), reads=[by, b_gb], writes=[by])
            k.dma("sp", dsy, lambda e: e.dma_start(out=d["x1"][t * 128:(t + 1) * 128, :], in_=y[:]), reads=[by], is_output=True)
        if not do_transpose:
            continue
        xT, bxT, dsxT = xTr[t % 2]
        for q4 in range(KC // 4):
            pt, bpt = psT.next()
            for j in range(4):
                dc = q4 * 4 + j
                k.op("pe", lambda e: e.transpose(pt[:, j * 128:(j + 1) * 128], y[:, dc * 128:(dc + 1) * 128], ident_f[:]), reads=[by, b_id], writes=[bpt])
            k.op("act", lambda e: e.activation(out=xT[:, q4 * 4:(q4 + 1) * 4, :], in_=pt[:].rearrange("p (c t) -> p c t", c=4), func=AF.Copy),
                 reads=[bpt], writes=[bxT])
        k.dma("sp", dsxT, lambda e: e.dma_start(out=d["x1T"][:, t * 128:(t + 1) * 128].rearrange("(c p) t -> p c t", p=128), in_=xT[:]),
              reads=[bxT], is_output=True)


def p5_dram(nc, NTOK, only_transpose=False):
    d = {}
    d["x"] = nc.dram_tensor("x", [NTOK, D_MODEL], F32, kind="ExternalInput").ap()
    if not only_transpose:
        d["mixT"] = nc.dram_tensor("mixT", [D_MODEL, NTOK], BF16, kind="ExternalInput").ap()
        d["w_out"] = nc.dram_tensor("w_out", [D_MODEL, D_MODEL], F32, kind="ExternalInput").ap()
        d["ln_g"] = nc.dram_tensor("ln_g", [1, D_MODEL], F32, kind="ExternalInput").ap()
        d["ln_b"] = nc.dram_tensor("ln_b", [1, D_MODEL], F32, kind="ExternalInput").ap()
        d["x1"] = nc.dram_tensor("x1", [NTOK, D_MODEL], F32, kind="ExternalOutput").ap()
    d["x1T"] = nc.dram_tensor("x1T", [D_MODEL, NTOK], BF16, kind="ExternalOutput").ap()
    return d


GROUPS = [[0, 1, 2, 3, 4, 5, 6, 7]]
CHUNK_MAP = []
for _g in range(4):
    CHUNK_MAP += [_g, 4 + 2 * _g, 5 + 2 * _g, 12 + _g]


def build_fused(S, L=2, B=2):
    NTOK = B * S // NCORES
    nc = bass.Bass("TRN2", target_bir_lowering=False)

    def ext(name, shape, dt):
        return nc.dram_tensor(name, list(shape), dt, kind="ExternalInput").ap()

    def itn(name, shape, dt, local=False):
        if local:
            return nc.dram_tensor(name, list(shape), dt, addr_space="Local", kind="Internal").ap()
        return nc.dram_tensor(name, list(shape), dt, kind="Internal").ap()

    x_in = ext("x", [NTOK, D_MODEL], F32)
    pos = ext("pos", [1, S], I32)
    out = nc.dram_tensor("out", [NTOK, D_MODEL], F32, kind="ExternalOutput").ap()
    xsend = itn("xsend", [D_MODEL, NTOK], BF16)
    xg = itn("xg", [8 * D_MODEL, NTOK], BF16, local=True)
    msend = itn("msend", [4 * 512, NTOK], BF16)
    mg = itn("mg", [8 * 2048, NTOK], BF16, local=True)
    x1mid = itn("x1mid", [NTOK, D_MODEL], F32)
    xb = itn("xb", [4 * D_MODEL, NTOK], BF16)
    mloc = itn("mloc", [2048, NTOK], BF16)

    def copy_x(k):
        ds = k.dsem("cpx")
        NPART = 4

        cache = {}

        def mk(j):
            def f(e):
                if "v" not in cache:
                    pid = e.partition_id()
                    b4 = pid - (pid % 4)
                    cache["v"] = xg[bass.ds(b4 * D_MODEL, 4 * D_MODEL), :]
                view = cache["v"]
                rows = 4 * D_MODEL // NPART
                return e.dma_start(out=xb[j * rows:(j + 1) * rows, :], in_=view[j * rows:(j + 1) * rows, :])
            return f
        for j in range(NPART):
            k.dma("sp", ds, mk(j), is_output=True)

    def copy_m(k):
        ds = k.dsem("cpm")
        NPART = 4

        cache = {}

        def mk(j):
            def f(e):
                if "v" not in cache:
                    pid = e.partition_id()
                    q = pid % 4
                    b4 = pid - q
                    cache["v"] = mg[bass.ds(b4 * 2048 + q * 512, 3 * 2048 + 512), :]
                view = cache["v"]
                return e.dma_start(out=mloc[j * 512:(j + 1) * 512, :], in_=view[j * 2048:j * 2048 + 512, :])
            return f
        for j in range(NPART):
            k.dma("act", ds, mk(j), is_output=True)
    sc = {
        "QnT": itn("QnT", [2, 128, S], BF16), "QpT": itn("QpT", [2, 64, S], BF16), "KnT": itn("KnT", [2, 128, S], BF16),
        "KpT": itn("KpT", [64, S], BF16), "V": itn("V", [2, 128, S // 128, 128], BF16), "SgM": itn("SgM", [2, 128, S], F32),
        "RW": itn("RW", [8, 128, S], F32),
    }

    nph = [0]
    stop = int(os.environ.get("FUSED_STOP", "999"))

    def phase(fn):
        nph[0] += 1
        if nph[0] > stop:
            return
        with contextlib.ExitStack() as es:
            k = KB(nc, es)
            fn(k)
            k.finish()

    phase(lambda k: build_p5(k, NTOK, {"x": x_in, "x1T": xsend}, only_transpose=True))
    phase(lambda k: k.cc("AllGather", GROUPS, xsend[:, :], xg[:, :]))
    phase(copy_x)
    for l in range(L):
        sfx = "_%d" % l
        w = {
            "w1": ext("w1" + sfx, [D_MODEL, P1_NC], F32), "pp": ext("pp" + sfx, [128, NPP_IN], F32),
            "wuq": ext("wuq" + sfx, [512, 512], F32), "wukv": ext("wukv" + sfx, [256, 512], F32),
            "w2a2": ext("w2a2" + sfx, [128, 128], F32), "w_out": ext("w_out" + sfx, [D_MODEL, D_MODEL], F32),
            "ln_g": ext("ln_g" + sfx, [1, D_MODEL], F32), "ln_b": ext("ln_b" + sfx, [1, D_MODEL], F32),
        }
        d1 = dict(sc)
        d1.update(xT=xb, xT_quarter=NTOK, w1=w["w1"], pp=w["pp"], wuq=w["wuq"], wukv=w["wukv"], w2a2=w["w2a2"], pos=pos, mix_q=(msend, NTOK))
        phase(lambda k: build_p1(k, S, d1))
        d3 = dict(sc)
        d3["mix_q"] = (msend, NTOK)
        phase(lambda k: build_p3(k, S, d3))
        d4 = {"RW": sc["RW"], "pp": w["pp"], "mix_q": (msend, NTOK)}
        phase(lambda k: build_p4(k, S, d4))
        phase(lambda k: k.cc("AllGather", GROUPS, msend[:, :], mg[:, :]))
        phase(copy_m)
        last = (l == L - 1)
        d5 = {"x": x_in if l == 0 else x1mid, "mixT": mloc, "w_out": w["w_out"], "ln_g": w["ln_g"], "ln_b": w["ln_b"],
              "x1": out if last else x1mid, "x1T": xsend}
        phase(lambda k: build_p5(k, NTOK, d5, chunk_map=CHUNK_MAP, do_transpose=not last))
        if not last:
            phase(lambda k: k.cc("AllGather", GROUPS, xsend[:, :], xg[:, :]))
            phase(copy_x)
    return nc


_FPROGS = {}


def fused_in_maps(inp):
    x = np.ascontiguousarray(inp["x"], dtype=np.float32)
    B, S, D = x.shape
    NTOK = B * S // NCORES
    xtok = x.reshape(NCORES, NTOK, D)
    L = inp["w_in"].shape[0]
    hps = [[host_layer_params(inp, l, g) for g in range(4)] for l in range(L)]
    maps = []
    for c in range(NCORES):
        b, g = c // 4, c % 4
        m = {"x": xtok[c], "pos": np.ascontiguousarray(inp["positions"][b:b + 1]).astype(np.int32)}
        for l in range(L):
            for n, v in hps[l][g].items():
                m["%s_%d" % (n, l)] = v
            m["w_out_%d" % l] = np.ascontiguousarray(inp["w_out"][l], dtype=np.float32)
            m["ln_g_%d" % l] = np.ascontiguousarray(inp["ln_g"][l][None, :], dtype=np.float32)
            m["ln_b_%d" % l] = np.ascontiguousarray(inp["ln_b"][l][None, :], dtype=np.float32)
        maps.append(m)
    return maps


def kernel_fused(**inp):
    inp = {k_: np.asarray(v) for k_, v in inp.items()}
    B, S, D = inp["x"].shape
    L = inp["w_in"].shape[0]
    key = ("fused", S, L, B)
    if key not in _FPROGS:
        _FPROGS[key] = build_fused(S, L, B)
    res = run_bass_kernel_spmd(_FPROGS[key], fused_in_maps(inp), core_ids=list(range(NCORES))).results
    outp = np.stack([np.asarray(res[c]["out"]) for c in range(NCORES)], axis=0)
    return outp.reshape(B, S, D).astype(np.float32)


_PROGS = {}


def _prog(name, S):
    key = (name, S)
    if key in _PROGS:
        return _PROGS[key]
    nc = bass.Bass("TRN2", target_bir_lowering=False)
    with contextlib.ExitStack() as es:
        k = KB(nc, es)
        if name == "p1":
            build_p1(k, S, p1_dram(nc, S))
        elif name == "p3":
            build_p3(k, S, p3_dram(nc, S))
        elif name == "p4":
            build_p4(k, S, p4_dram(nc, S))
        elif name == "p5":
            build_p5(k, S, p5_dram(nc, S))
        elif name == "p0":
            build_p5(k, S, p5_dram(nc, S, only_transpose=True), only_transpose=True)
        k.finish()
    _PROGS[key] = nc
    return nc


def _run(nc, in_maps):
    return run_bass_kernel_spmd(nc, in_maps, core_ids=list(range(NCORES))).results


def kernel(**inp):
    inp = {k_: np.asarray(v) for k_, v in inp.items()}
    x = np.ascontiguousarray(inp["x"], dtype=np.float32)
    B, S, D = x.shape
    NTOK = B * S // NCORES
    xtok = x.reshape(NCORES, NTOK, D)
    res = _run(_prog("p0", NTOK), [{"x": xtok[c]} for c in range(NCORES)])
    xT = [np.concatenate([np.asarray(res[4 * b + q]["x1T"]) for q in range(4)], axis=1) for b in range(B)]
    cur = xtok
    L = inp["w_in"].shape[0]
    for l in range(L):
        hps = [host_layer_params(inp, l, g) for g in range(4)]
        maps = []
        for c in range(NCORES):
            b, g = c // 4, c % 4
            m = dict(hps[g])
            m["xT"] = xT[b]
            m["pos"] = np.ascontiguousarray(inp["positions"][b:b + 1]).astype(np.int32)
            maps.append(m)
        r1 = _run(_prog("p1", S), maps)
        r3 = _run(_prog("p3", S), [{n: np.asarray(r1[c][n]) for n in ("QnT", "QpT", "KnT", "KpT", "V", "SgM")} for c in range(NCORES)])
        r4 = _run(_prog("p4", S), [{"RW": np.asarray(r1[c]["RW"]), "pp": hps[c % 4]["pp"]} for c in range(NCORES)])
        mixT = []
        for b in range(B):
            rows = [np.asarray(r1[4 * b + g]["ycT"]) for g in range(4)]
            for g in range(4):
                ym = np.asarray(r3[4 * b + g]["ymT"])
                rows += [ym[0], ym[1]]
            rows += [np.asarray(r4[4 * b + g]["yrT"]) for g in range(4)]
            mixT.append(np.concatenate(rows, axis=0))
        maps = []
        for c in range(NCORES):
            b, q = c // 4, c % 4
            maps.append({"x": np.ascontiguousarray(cur[c]), "mixT": np.ascontiguousarray(mixT[b][:, q * NTOK:(q + 1) * NTOK]),
                         "w_out": np.ascontiguousarray(inp["w_out"][l]), "ln_g": np.ascontiguousarray(inp["ln_g"][l][None, :]),
                         "ln_b": np.ascontiguousarray(inp["ln_b"][l][None, :])})
        r5 = _run(_prog("p5", NTOK), maps)
        cur = np.stack([np.asarray(r5[c]["x1"]) for c in range(NCORES)], axis=0)
        xT = [np.concatenate([np.asarray(r5[4 * b + q]["x1T"]) for q in range(4)], axis=1) for b in range(B)]
    return cur.reshape(B, S, D).astype(np.float32)
```
